# Optimizing a Trainium2 kernel written in Bass

```python
import jax, jax.numpy as jnp
from jax import lax
import numpy as np

D_MODEL = 1024
BATCH = 8
SEQ = 4096
DEPTH = 1
DEC_BATCH = 1
DEC_SEQ = 16384
PAST_LEN = 128

N_META = 16
GRID_W = 64
HEAD_DIM = 64
NA_HEADS = 8
NA_WIN_H = 8
NA_WIN_W = 16
GQA_HEADS = 8
GQA_KV_HEADS = 2
GQA_GROUP = GQA_HEADS // GQA_KV_HEADS
Q_BLOCK = 128
W_A = NA_HEADS * HEAD_DIM
W_B = GQA_HEADS * HEAD_DIM
W_KV = GQA_KV_HEADS * HEAD_DIM
MIX_WIDTH = W_A + W_B
IN_COLS = 3 * W_A + W_B + 2 * W_KV
IN_SPLITS = (W_A, 2 * W_A, 3 * W_A, 3 * W_A + W_B, 3 * W_A + W_B + W_KV)
D_FF = 2816
ROPE_THETA = 10000.0
ROPE_AXIS_DIM = HEAD_DIM // 2
EPS = 1e-6

kernel_name = 'hybrid_natten_axial_gqa_encoder'


def _rmsnorm(x, g):
    xf = x.astype(jnp.float32)
    xf = xf * lax.rsqrt(jnp.mean(xf * xf, axis=-1, keepdims=True) + EPS)
    return (xf * g.astype(jnp.float32)).astype(x.dtype)


def _swiglu(x, w_gate, w_up, w_down):
    return (jax.nn.silu(x @ w_gate) * (x @ w_up)) @ w_down


def _half_ffn(x, norm_pre, w_gate, w_up, w_down, norm_post):
    return x + 0.5 * _rmsnorm(_swiglu(_rmsnorm(x, norm_pre), w_gate, w_up, w_down), norm_post)


def _neighbourhood_attention(q, k, v, rel_bias, meta_bias):
    b, l, h, d = q.shape
    rows = (l - N_META) // GRID_W
    kh = min(NA_WIN_H, rows)
    kw = NA_WIN_W
    scale = HEAD_DIM ** -0.5
    qm, km, vm = q[:, :N_META], k[:, :N_META], v[:, :N_META]
    qg = q[:, N_META:].reshape(b, rows, GRID_W, h, d)
    kg = k[:, N_META:].reshape(b, rows, GRID_W, h, d)
    vg = v[:, N_META:].reshape(b, rows, GRID_W, h, d)
    mb = meta_bias.astype(jnp.float32)[None, :, None, :]
    s_mm = jnp.einsum('bqhd,bkhd->bhqk', qm, km).astype(jnp.float32) * scale + mb
    p_mm = jax.nn.softmax(s_mm, axis=-1).astype(v.dtype)
    y_meta = jnp.einsum('bhqk,bkhd->bqhd', p_mm, vm)
    col = np.arange(GRID_W)
    col_start = np.clip(col - kw // 2, 0, GRID_W - kw)
    col_idx = col_start[:, None] + np.arange(kw)[None, :]
    col_bias_idx = col_idx - col[:, None] + (NA_WIN_W - 1)

    def row_block(r):
        r0 = jnp.clip(r - kh // 2, 0, rows - kh)
        q_r = lax.dynamic_index_in_dim(qg, r, axis=1, keepdims=False)
        k_win = lax.dynamic_slice_in_dim(kg, r0, kh, axis=1)[:, :, col_idx]
        v_win = lax.dynamic_slice_in_dim(vg, r0, kh, axis=1)[:, :, col_idx]
        row_bias_idx = r0 + jnp.arange(kh) - r + (NA_WIN_H - 1)
        bias = rel_bias[:, row_bias_idx][:, :, col_bias_idx]
        s_win = jnp.einsum('bqhd,brqkhd->bhqrk', q_r, k_win).astype(jnp.float32) * scale
        s_win = s_win + jnp.transpose(bias, (0, 2, 1, 3))[None].astype(jnp.float32)
        s_meta = jnp.einsum('bqhd,bkhd->bhqk', q_r, km).astype(jnp.float32) * scale + mb
        s = jnp.concatenate([s_win.reshape(b, h, GRID_W, kh * kw), s_meta], axis=-1)
        p = jax.nn.softmax(s, axis=-1).astype(v.dtype)
        p_win = p[..., :kh * kw].reshape(b, h, GRID_W, kh, kw)
        return (jnp.einsum('bhqrk,brqkhd->bqhd', p_win, v_win)
                + jnp.einsum('bhqk,bkhd->bqhd', p[..., kh * kw:], vm))

    y_grid = lax.map(row_block, jnp.arange(rows))
    y_grid = jnp.transpose(y_grid, (1, 0, 2, 3, 4)).reshape(b, rows * GRID_W, h * d)
    return jnp.concatenate([y_meta.reshape(b, N_META, h * d), y_grid], axis=1)


def _rotate_half(x, ang):
    n = ang.shape[-1]
    cos = jnp.cos(ang)[:, None, :].astype(x.dtype)
    sin = jnp.sin(ang)[:, None, :].astype(x.dtype)
    x1, x2 = x[..., :n], x[..., n:]
    return jnp.concatenate([x1 * cos - x2 * sin, x1 * sin + x2 * cos], axis=-1)


def _axial_rope(x, ang_row, ang_col):
    half = HEAD_DIM // 2
    return jnp.concatenate([_rotate_half(x[..., :half], ang_row),
                            _rotate_half(x[..., half:], ang_col)], axis=-1)


def _axial_angles(n_tokens):
    t = jnp.arange(n_tokens)
    pos_row = jnp.concatenate([-jnp.ones((N_META,), jnp.float32), (t // GRID_W).astype(jnp.float32)])
    pos_col = jnp.concatenate([jnp.arange(N_META, dtype=jnp.float32), (t % GRID_W).astype(jnp.float32)])
    inv_freq = jnp.asarray(ROPE_THETA ** (-np.arange(0, ROPE_AXIS_DIM, 2) / ROPE_AXIS_DIM), jnp.float32)
    return pos_row[:, None] * inv_freq[None, :], pos_col[:, None] * inv_freq[None, :]


def _axial_gqa(q, k, v, q_norm, k_norm):
    b, l, _, d = q.shape
    s = l - N_META
    ang_row, ang_col = _axial_angles(s)
    q = _axial_rope(_rmsnorm(q, q_norm), ang_row, ang_col) * (HEAD_DIM ** -0.5)
    k = _axial_rope(_rmsnorm(k, k_norm), ang_row, ang_col)
    q = q.reshape(b, l, GQA_KV_HEADS, GQA_GROUP, d)

    def attend(qb):
        sc = jnp.einsum('bqngd,bsnd->bngqs', qb, k).astype(jnp.float32)
        p = jax.nn.softmax(sc, axis=-1).astype(v.dtype)
        return jnp.einsum('bngqs,bsnd->bqngd', p, v)

    y_meta = attend(q[:, :N_META]).reshape(b, N_META, GQA_HEADS * d)
    qb = jnp.transpose(q[:, N_META:].reshape(b, s // Q_BLOCK, Q_BLOCK, GQA_KV_HEADS, GQA_GROUP, d),
                       (1, 0, 2, 3, 4, 5))
    y_grid = lax.map(attend, qb)
    y_grid = jnp.transpose(y_grid, (1, 0, 2, 3, 4, 5)).reshape(b, s, GQA_HEADS * d)
    return jnp.concatenate([y_meta, y_grid], axis=1)


def _token_mixer(x, norm_pre, w_in, na_rel_bias, na_meta_bias, gqa_q_norm, gqa_k_norm,
                 grp_norm_a, grp_norm_b, w_out, norm_post):
    b, l, _ = x.shape
    h = _rmsnorm(x, norm_pre)
    proj = h @ w_in
    qa, ka, va, qb, kb, vb = jnp.split(proj, list(IN_SPLITS), axis=-1)
    shp_a = (b, l, NA_HEADS, HEAD_DIM)
    shp_kv = (b, l, GQA_KV_HEADS, HEAD_DIM)
    ya = _neighbourhood_attention(qa.reshape(shp_a), ka.reshape(shp_a), va.reshape(shp_a),
                                  na_rel_bias, na_meta_bias)
    yb = _axial_gqa(qb.reshape(b, l, GQA_HEADS, HEAD_DIM), kb.reshape(shp_kv), vb.reshape(shp_kv),
                    gqa_q_norm, gqa_k_norm)
    y = jnp.concatenate([_rmsnorm(ya, grp_norm_a), _rmsnorm(yb, grp_norm_b)], axis=-1) @ w_out
    return x + _rmsnorm(y, norm_post)


def setup_inputs(seed: int = 0) -> dict:
    key = jax.random.key(seed)
    ks = jax.random.split(key, 24)

    def nrm(k, shape, scale):
        return scale * jax.random.normal(k, shape, jnp.float32)

    def gain(k, shape):
        return 1.0 + 0.05 * jax.random.normal(k, shape, jnp.float32)

    return {
        'x_prompt': nrm(ks[0], (BATCH, SEQ, D_MODEL), 1.0),
        'x_sample': nrm(ks[1], (DEC_BATCH, DEC_SEQ, D_MODEL), 1.0),
        'meta_tokens': nrm(ks[2], (N_META, D_MODEL), 1.0),
        'ffn1_norm_pre': gain(ks[3], (DEPTH, D_MODEL)),
        'ffn1_w_gate': nrm(ks[4], (DEPTH, D_MODEL, D_FF), D_MODEL ** -0.5),
        'ffn1_w_up': nrm(ks[5], (DEPTH, D_MODEL, D_FF), D_MODEL ** -0.5),
        'ffn1_w_down': nrm(ks[6], (DEPTH, D_FF, D_MODEL), D_FF ** -0.5),
        'ffn1_norm_post': gain(ks[7], (DEPTH, D_MODEL)),
        'mix_norm_pre': gain(ks[8], (DEPTH, D_MODEL)),
        'w_in': nrm(ks[9], (DEPTH, D_MODEL, IN_COLS), D_MODEL ** -0.5),
        'na_rel_bias': nrm(ks[10], (DEPTH, NA_HEADS, 2 * NA_WIN_H - 1, 2 * NA_WIN_W - 1), 0.02),
        'na_meta_bias': nrm(ks[11], (DEPTH, NA_HEADS, N_META), 0.02),
        'gqa_q_norm': gain(ks[12], (DEPTH, HEAD_DIM)),
        'gqa_k_norm': gain(ks[13], (DEPTH, HEAD_DIM)),
        'grp_norm_a': gain(ks[14], (DEPTH, W_A)),
        'grp_norm_b': gain(ks[15], (DEPTH, W_B)),
        'w_out': nrm(ks[16], (DEPTH, MIX_WIDTH, D_MODEL), MIX_WIDTH ** -0.5),
        'mix_norm_post': gain(ks[17], (DEPTH, D_MODEL)),
        'ffn2_norm_pre': gain(ks[18], (DEPTH, D_MODEL)),
        'ffn2_w_gate': nrm(ks[19], (DEPTH, D_MODEL, D_FF), D_MODEL ** -0.5),
        'ffn2_w_up': nrm(ks[20], (DEPTH, D_MODEL, D_FF), D_MODEL ** -0.5),
        'ffn2_w_down': nrm(ks[21], (DEPTH, D_FF, D_MODEL), D_FF ** -0.5),
        'ffn2_norm_post': gain(ks[22], (DEPTH, D_MODEL)),
        'final_norm': gain(ks[23], (D_MODEL,)),
    }


def reference(x_prompt, x_sample, meta_tokens,
              ffn1_norm_pre, ffn1_w_gate, ffn1_w_up, ffn1_w_down, ffn1_norm_post,
              mix_norm_pre, w_in, na_rel_bias, na_meta_bias, gqa_q_norm, gqa_k_norm,
              grp_norm_a, grp_norm_b, w_out, mix_norm_post,
              ffn2_norm_pre, ffn2_w_gate, ffn2_w_up, ffn2_w_down, ffn2_norm_post,
              final_norm):
    def run(x):
        b = x.shape[0]
        meta = jnp.broadcast_to(meta_tokens.astype(x.dtype)[None], (b, N_META, D_MODEL))
        h = jnp.concatenate([meta, x], axis=1)
        for l in range(DEPTH):
            h = _half_ffn(h, ffn1_norm_pre[l], ffn1_w_gate[l], ffn1_w_up[l], ffn1_w_down[l], ffn1_norm_post[l])
            h = _token_mixer(h, mix_norm_pre[l], w_in[l], na_rel_bias[l], na_meta_bias[l],
                             gqa_q_norm[l], gqa_k_norm[l], grp_norm_a[l], grp_norm_b[l],
                             w_out[l], mix_norm_post[l])
            h = _half_ffn(h, ffn2_norm_pre[l], ffn2_w_gate[l], ffn2_w_up[l], ffn2_w_down[l], ffn2_norm_post[l])
        return _rmsnorm(h, final_norm)[:, N_META:]

    y_prompt = run(x_prompt)
    y_sample = run(x_sample)
    return (y_prompt, y_sample)
```

```python
import os
import numpy as np
import ml_dtypes
from contextlib import ExitStack
import concourse.bass as bass
import concourse.mybir as mybir
from concourse.bass_utils import run_bass_kernel_spmd

F32 = mybir.dt.float32
BF16 = mybir.dt.bfloat16
AF = mybir.ActivationFunctionType
ALU = mybir.AluOpType
AX = mybir.AxisListType

D = 1024
DC = 8
FF = 2816
FC = 22
NPR = 4096
NSC = 2560
NQ = NPR + NSC
NOWN = NQ + 128
NSF = 16384 - 2048
EPS = 1e-6
NEG = -30000.0
ARENA = 103000


class Buf:
    __slots__ = ("w", "rs")

    def __init__(self):
        self.w = None
        self.rs = []


class Eng:
    def __init__(self, name, sem, is_pe=False):
        self.name = name
        self.sem = sem
        self.n = 0
        self.seen = {}
        self.prog = []
        self.is_pe = is_pe
        self.pool = None
        self.pool_i = 0

    def _deps(self, reads, writes):
        toks = []
        for b in reads:
            if b.w is not None:
                toks.append(b.w)
        for b in writes:
            if b.w is not None:
                toks.append(b.w)
            toks.extend(b.rs)
        waits = []
        for (s, v) in toks:
            if self.is_pe and s is self.sem:
                continue
            if self.seen.get(id(s), 0) >= v:
                continue
            self.seen[id(s)] = v
            waits.append((s, v))
        return waits

    @staticmethod
    def _mark(tok, reads, writes):
        for b in reads:
            b.rs.append(tok)
        for b in writes:
            b.w = tok
            b.rs = []

    def op(self, fn, reads=(), writes=()):
        waits = self._deps(reads, writes)
        self.n += 1
        sem = self.sem

        def emit(e, waits=waits, fn=fn, sem=sem):
            for (s, v) in waits:
                e.wait_ge(s, v)
            fn(e).then_inc(sem, 1)
        self.prog.append(emit)
        tok = (sem, self.n)
        self._mark(tok, reads, writes)
        return tok

    def dma(self, out, in_, reads=(), writes=(), slow=False):
        waits = self._deps(reads, writes)
        ent = self.pool[self.pool_i % len(self.pool)]
        self.pool_i += 1
        sem, prev = ent
        if prev > 0 and self.seen.get(id(sem), 0) < prev:
            self.seen[id(sem)] = prev
            waits.append((sem, prev))
        ent[1] = prev + 16

        def emit(e, waits=waits, out=out, in_=in_, sem=sem, slow=slow):
            for (s, v) in waits:
                e.wait_ge(s, v)
            if slow:
                e.dma_start(out=out, in_=in_, allow_slow_non_contiguous=True).then_inc(sem, 16)
            else:
                e.dma_start(out=out, in_=in_).then_inc(sem, 16)
        self.prog.append(emit)
        tok = (sem, prev + 16)
        self._mark(tok, reads, writes)
        return tok


class K:
    pass


def _na_tables(rel_bias, core):
    rb = np.asarray(rel_bias, np.float32)

    def build(qrows, krow0, rows, kvalid_fn):
        tab = np.full((8, 128, 6, 256), NEG, np.float32)
        kc = np.arange(64)
        qc = np.arange(64)
        c0 = np.clip(qc - 8, 0, 48)
        colok = (kc[:, None] >= c0[None, :]) & (kc[:, None] < c0[None, :] + 16)
        dc = np.clip(kc[:, None] - qc[None, :] + 15, 0, 30)
        for o in range(6):
            for krl in range(2):
                kr = krow0 + 2 * o + krl
                if not kvalid_fn(kr):
                    continue
                for qi, qr in enumerate(qrows):
                    r0 = min(max(qr - 4, 0), rows - 8)
                    if not (r0 <= kr < r0 + 8):
                        continue
                    dr = kr - qr + 7
                    vals = rb[:, dr, :][:, dc]
                    vals = np.where(colok[None], vals, np.float32(NEG))
                    tab[:, krl * 64:(krl + 1) * 64, o, qi * 64:(qi + 1) * 64] = vals
        return tab

    t_int = build([8, 9, 10, 11], 4, 64, lambda r: True)
    t_pf = build([0, 1, 2, 3], 0, 64, lambda r: True)
    t_pl = build([60, 61, 62, 63], 52, 64, lambda r: True)
    g0 = 32 * core - 4
    okr = lambda r: 0 <= r < 256
    t_sf = build([g0 + 4 + i for i in range(4)], g0, 256, okr)
    t_sl = build([g0 + 32 + i for i in range(4)], g0 + 28, 256, okr)
    return np.stack([t_int, t_pf, t_pl, t_sf, t_sl]).reshape(5, 8, 128, 1536)


def _rope_table(pos_row, pos_col):
    inv = (10000.0 ** (-np.arange(0, 32, 2) / 32)).astype(np.float32)
    ar = pos_row.astype(np.float32)[:, None] * inv[None, :]
    ac = pos_col.astype(np.float32)[:, None] * inv[None, :]
    cr, sr, cc, sc = np.cos(ar), np.sin(ar), np.cos(ac), np.sin(ac)
    C = np.concatenate([cr, cr, cc, cc], 1)
    S = np.concatenate([-sr, sr, -sc, sc], 1)
    return np.concatenate([C, S], 1).astype(np.float32)


def build_program():
    nc = bass.Bass("TRN2", target_bir_lowering=False)
    k = K()
    dr = {}

    def din(name, shape, dt=F32):
        dr[name] = nc.dram_tensor(name, list(shape), dt, kind="ExternalInput").ap()

    def dscr(name, shape, dt):
        dr[name] = nc.dram_tensor(name, list(shape), dt).ap()

    din("x_own", [NOWN, D]); din("xs_full", [NSF, D])
    din("rope_own", [NOWN, 128]); din("rope_full", [NSF, 128])
    din("na_tab", [5, 8, 128, 1536])
    for f in ("ffn1", "ffn2"):
        din(f + "_wg", [D, FF]); din(f + "_wu", [D, FF]); din(f + "_wd", [FF, D])
        din(f + "_pre", [D]); din(f + "_post", [D])
    din("w_in", [D, 2304]); din("w_out", [D, D])
    din("mix_pre", [D]); din("mix_post", [D]); din("final_norm", [D])
    din("q_norm", [64]); din("k_norm", [64]); din("grp_ab", [D]); din("meta_bias", [8, 16])
    y_p = nc.dram_tensor("y_p", [NPR, D], F32, kind="ExternalOutput").ap()
    y_s = nc.dram_tensor("y_s", [2048, D], F32, kind="ExternalOutput").ap()
    dscr("h1_own", [NOWN, D], F32); dscr("h1_full", [NSF, D], F32)
    dscr("qaT", [128, 4, NQ], BF16); dscr("kaT", [128, 4, NOWN], BF16)
    dscr("va", [NOWN, 640], BF16)
    dscr("qbT", [128, 4, NQ], BF16); dscr("kbT_own", [128, NOWN], BF16)
    dscr("vb_own", [NOWN, 160], BF16)
    dscr("kbT_full", [128, NSF], BF16); dscr("vb_full", [NSF, 160], BF16)
    dscr("ycat", [NQ, D], BF16); dscr("h2", [NQ, D], F32)

    es = ExitStack()
    arena = es.enter_context(nc.sbuf_tensor("arena", [128, ARENA], BF16))
    big = [es.enter_context(nc.psum_tensor("pbig%d" % i, [128, 1024], F32)) for i in range(4)]
    banks = [big[i // 2][:, (i % 2) * 512:(i % 2 + 1) * 512] for i in range(8)]
    bankb = [Buf() for _ in range(8)]

    def sem(name):
        return es.enter_context(nc.semaphore(name))

    pe = Eng("pe", sem("s_pe"), is_pe=True)
    act = Eng("act", sem("s_act"))
    dve = Eng("dve", sem("s_dve"))
    pool = Eng("pool", sem("s_pool"))
    sp = Eng("sp", sem("s_sp"))
    sp.pool = [[sem("dq_sp%d" % i), 0] for i in range(16)]
    pool.pool = [[sem("dq_pl%d" % i), 0] for i in range(16)]
    engines = [pe, act, dve, pool, sp]

    st = {"off": 0}

    def alloc(shape, dt):
        n = int(np.prod(shape[1:]))
        nb = n * (4 if dt == F32 else 2)
        nb = (nb + 63) // 64 * 64
        off = st["off"]
        st["off"] += nb // 2
        assert st["off"] <= ARENA, ("arena overflow", st["off"])
        v = arena[0:shape[0], off:off + nb // 2]
        if dt == F32:
            v = v.bitcast(F32)
        v = v[:, 0:n]
        if len(shape) == 3:
            v = v.rearrange("p (a b) -> p a b", a=shape[1])
        elif len(shape) == 4:
            v = v.rearrange("p (a b c) -> p a b c", a=shape[1], b=shape[2])
        return v

    def barrier():
        toks = []
        for e in engines:
            if e.n > 0:
                toks.append((e.sem, e.n))
            if e.pool:
                for (s, v) in e.pool:
                    if v > 0:
                        toks.append((s, v))
        for e in engines:
            waits = []
            for (s, v) in toks:
                if e.is_pe and s is e.sem:
                    continue
                if e.seen.get(id(s), 0) >= v:
                    continue
                e.seen[id(s)] = v
                waits.append((s, v))

            def emit(eng, waits=waits):
                for (s, v) in waits:
                    eng.wait_ge(s, v)
            e.prog.append(emit)

    ident_b = alloc([128, 128], BF16)
    ident_f = alloc([128, 128], F32)
    identB = Buf()
    pool.op(lambda e: e.memset(ident_f, 0.0), writes=[identB])
    pool.op(lambda e: e.affine_select(out=ident_f, in_=ident_f, pattern=[[-1, 128]],
                                      compare_op=ALU.not_equal, fill=1.0, base=0,
                                      channel_multiplier=1), reads=[identB], writes=[identB])
    pool.op(lambda e: e.tensor_copy(out=ident_b, in_=ident_f), reads=[identB], writes=[identB])
    base_off = st["off"]

    def colgain(name):
        t = alloc([128, 8], F32)
        t8 = alloc([8, 128], F32)
        b = Buf(); b8 = Buf()
        sp.dma(t8, dr[name].rearrange("(c p) -> c p", p=128), writes=[b8])
        pe.op(lambda e: e.transpose(banks[0][:, 0:8], t8, ident_f[0:8, 0:8]), reads=[b8, identB], writes=[bankb[0]])
        act.op(lambda e: e.activation(out=t, in_=banks[0][:, 0:8], func=AF.Copy), reads=[bankb[0]], writes=[b])
        return t, b

    def rowgain(name):
        t = alloc([128, D], F32)
        b = Buf()
        sp.dma(t, dr[name].partition_broadcast(128), writes=[b])
        return t, b

    def rstd_ops(ss, out, n, invn, rb, wb):
        act.op(lambda e: e.activation(out=out[:, 0:n], in_=ss[:, 0:n], func=AF.Ln, scale=invn, bias=EPS),
               reads=rb, writes=wb)
        act.op(lambda e: e.activation(out=out[:, 0:n], in_=out[:, 0:n], func=AF.Exp, scale=-0.5),
               reads=wb, writes=wb)

    def load_weight(dst, src, nchunk, b):
        for c in range(nchunk):
            pool.dma(dst[:, c, :], src[c * 128:(c + 1) * 128, :], writes=[b])

    def norm_transpose(r, rB, ns, xn, xnB, xnT, xnTB, gcol, gcolB, ss, rs, sB, junk, junkB, pb):
        for s in range(ns):
            act.op(lambda e, s=s: e.activation(out=junk, in_=r[:, s, :], func=AF.Square,
                                               accum_out=ss[:, s:s + 1]),
                   reads=[rB[s]], writes=[junkB, sB])
        rstd_ops(ss, rs, ns, 1.0 / D, [sB], [sB])
        for s in range(ns):
            dve.op(lambda e, s=s: e.tensor_scalar(out=xn[:, s, :], in0=r[:, s, :], scalar1=rs[:, s:s + 1],
                                                  scalar2=None, op0=ALU.mult),
                   reads=[rB[s], sB], writes=[xnB[s]])
        for dc in range(DC):
            bi = pb[dc % 2]
            pt = banks[bi][:].bitcast(BF16)

            def tr(e, dc=dc, pt=pt):
                ins = None
                for s in range(ns):
                    ins = e.transpose(pt[:, s * 128:(s + 1) * 128], xn[:, s, dc * 128:(dc + 1) * 128], ident_b)
                return ins
            pe.op(tr, reads=[xnB[s] for s in range(ns)] + [identB], writes=[bankb[bi]])
            if dc % 2 == 0:
                act.op(lambda e, dc=dc, pt=pt: e.activation(out=xnT[:, dc, 0:ns * 128], in_=pt[:, 0:ns * 128],
                                                            func=AF.Copy, scale=gcol[:, dc:dc + 1]),
                       reads=[bankb[bi], gcolB], writes=[xnTB])
            else:
                dve.op(lambda e, dc=dc, pt=pt: e.tensor_scalar(out=xnT[:, dc, 0:ns * 128], in0=pt[:, 0:ns * 128],
                                                               scalar1=gcol[:, dc:dc + 1], scalar2=None, op0=ALU.mult),
                       reads=[bankb[bi], gcolB], writes=[xnTB])

    def post_resid(pyb, r, rB, s, ysb, ysbB, junk, junkB, ss, rs, sB, grow, growB, half):
        py = banks[pyb[0]]
        py1 = banks[pyb[1]]
        act.op(lambda e: e.activation(out=ysb[:, 0:512], in_=py[:], func=AF.Copy), reads=[bankb[pyb[0]]], writes=[ysbB])
        dve.op(lambda e: e.tensor_copy(out=ysb[:, 512:1024], in_=py1[:]), reads=[bankb[pyb[1]]], writes=[ysbB])
        act.op(lambda e: e.activation(out=junk, in_=ysb, func=AF.Square, accum_out=ss[:, 0:1]),
               reads=[ysbB], writes=[junkB, sB])
        rstd_ops(ss, rs, 1, 1.0 / D, [sB], [sB])
        dve.op(lambda e: e.scalar_tensor_tensor(out=ysb, in0=ysb, scalar=rs[:, 0:1], in1=grow,
                                                op0=ALU.mult, op1=ALU.mult),
               reads=[ysbB, sB, growB], writes=[ysbB])
        dve.op(lambda e: e.scalar_tensor_tensor(out=r[:, s, :], in0=ysb, scalar=half, in1=r[:, s, :],
                                                op0=ALU.mult, op1=ALU.add),
               reads=[ysbB, rB[s]], writes=[rB[s]])

    def ffn_phase(tiles, pfx, final):
        st["off"] = base_off
        wg = alloc([128, DC, FF], BF16); wu = alloc([128, DC, FF], BF16); wd = alloc([128, FC, D], BF16)
        wgB, wuB, wdB = Buf(), Buf(), Buf()
        load_weight(wg, dr[pfx + "_wg"], DC, wgB)
        load_weight(wu, dr[pfx + "_wu"], DC, wuB)
        load_weight(wd, dr[pfx + "_wd"], FC, wdB)
        gpre, gpreB = colgain(pfx + "_pre")
        gpost, gpostB = rowgain(pfx + "_post")
        if final:
            gfin, gfinB = rowgain("final_norm")
        r = alloc([128, 4, D], F32); rB = [Buf() for _ in range(4)]
        xn = alloc([128, 4, D], BF16); xnB = [Buf() for _ in range(4)]
        xnB[1] = xnB[0]
        ysb = xn[:, 0:2, :].rearrange("p a b -> p (a b)").bitcast(F32)
        ysbB = xnB[0]
        xnT = alloc([128, DC, 512], BF16); xnTB = Buf()
        actT = alloc([128, FC, 512], BF16); actTB = Buf()
        sg = alloc([128, 512], F32); sgB = Buf()
        junk = alloc([128, D], BF16); junkB = Buf()
        ss = alloc([128, 4], F32); rs = alloc([128, 4], F32); sB = Buf()
        ss2 = alloc([128, 4], F32); rs2 = alloc([128, 4], F32); s2B = Buf()
        for ti, (src, dst, ns) in enumerate(tiles):
            N = ns * 128
            if ti == 0:
                sp.dma(r[:, 0:ns, :], src.rearrange("(s p) d -> p s d", p=128), writes=rB[0:ns])
            nxt = tiles[ti + 1] if ti + 1 < len(tiles) else None
            norm_transpose(r, rB, ns, xn, xnB, xnT, xnTB, gpre, gpreB, ss, rs, sB, junk, junkB, (0, 1))
            for fc in range(FC):
                bg = 2 + 2 * (fc % 2)
                bu = bg + 1

                def mm(e, fc=fc, bg=bg, bu=bu, N=N):
                    ins = None
                    for dc in range(DC):
                        ins = e.matmul(banks[bg][:, 0:N], lhsT=wg[:, dc, fc * 128:(fc + 1) * 128], rhs=xnT[:, dc, 0:N],
                                       start=(dc == 0), stop=(dc == DC - 1))
                    for dc in range(DC):
                        ins = e.matmul(banks[bu][:, 0:N], lhsT=wu[:, dc, fc * 128:(fc + 1) * 128], rhs=xnT[:, dc, 0:N],
                                       start=(dc == 0), stop=(dc == DC - 1))
                    return ins
                pe.op(mm, reads=[wgB, wuB, xnTB], writes=[bankb[bg], bankb[bu]])
                act.op(lambda e, bg=bg, N=N: e.activation(out=sg[:, 0:N], in_=banks[bg][:, 0:N], func=AF.Silu),
                       reads=[bankb[bg]], writes=[sgB])
                dve.op(lambda e, fc=fc, bu=bu, N=N: e.tensor_tensor(out=actT[:, fc, 0:N], in0=sg[:, 0:N],
                                                                   in1=banks[bu][:, 0:N], op=ALU.mult),
                       reads=[sgB, bankb[bu]], writes=[actTB])
            for s in range(ns):
                pyb = (6, 7)

                def dn(e, s=s):
                    ins = None
                    for hf in range(2):
                        for fc in range(FC):
                            ins = e.matmul(banks[6 + hf][:], lhsT=actT[:, fc, s * 128:(s + 1) * 128],
                                           rhs=wd[:, fc, hf * 512:(hf + 1) * 512], start=(fc == 0), stop=(fc == FC - 1))
                    return ins
                pe.op(dn, reads=[actTB, wdB], writes=[bankb[6], bankb[7]])
                post_resid(pyb, r, rB, s, ysb, ysbB, junk, junkB, ss2, rs2, s2B, gpost, gpostB, 0.5)
                if final:
                    act.op(lambda e, s=s: e.activation(out=junk, in_=r[:, s, :], func=AF.Square, accum_out=ss2[:, 1:2]),
                           reads=[rB[s]], writes=[junkB, s2B])
                    act.op(lambda e: e.activation(out=rs2[:, 1:2], in_=ss2[:, 1:2], func=AF.Ln, scale=1.0 / D, bias=EPS),
                           reads=[s2B], writes=[s2B])
                    act.op(lambda e: e.activation(out=rs2[:, 1:2], in_=rs2[:, 1:2], func=AF.Exp, scale=-0.5),
                           reads=[s2B], writes=[s2B])
                    dve.op(lambda e, s=s: e.scalar_tensor_tensor(out=r[:, s, :], in0=r[:, s, :], scalar=rs2[:, 1:2],
                                                                 in1=gfin, op0=ALU.mult, op1=ALU.mult),
                           reads=[rB[s], s2B, gfinB], writes=[rB[s]])
                sp.dma(dst[s * 128:(s + 1) * 128, :], r[:, s, :], reads=[rB[s]])
                if nxt is not None and s < nxt[2]:
                    sp.dma(r[:, s, :], nxt[0][s * 128:(s + 1) * 128, :], writes=[rB[s]])
            if nxt is not None:
                for s in range(ns, nxt[2]):
                    sp.dma(r[:, s, :], nxt[0][s * 128:(s + 1) * 128, :], writes=[rB[s]])
        barrier()

    def proj_phase():
        st["off"] = base_off
        win = alloc([128, DC, 2304], BF16); winB = Buf()
        wi = dr["w_in"]
        for c in range(DC):
            rows = wi[c * 128:(c + 1) * 128, :]
            pool.dma(win[:, c, 0:1536], rows[:, 0:1536], writes=[winB])
            srcq = rows[:, 1536:2048].rearrange("p (g j d) -> p j g d", g=2, j=4)
            for j in range(4):
                pool.dma(win[:, c, 1536 + j * 128:1536 + (j + 1) * 128].rearrange("p (g d) -> p g d", g=2),
                         srcq[:, j], writes=[winB])
            pool.dma(win[:, c, 2048:2304], rows[:, 2048:2304], writes=[winB])
        gmix, gmixB = colgain("mix_pre")
        gqk = alloc([128, 10, 64], F32); gqkB = Buf()
        g2 = alloc([128, 2, 64], F32); g2B = Buf()
        sp.dma(g2[:, 0, :], dr["q_norm"].partition_broadcast(128), writes=[g2B])
        sp.dma(g2[:, 1, :], dr["k_norm"].partition_broadcast(128), writes=[g2B])
        dve.op(lambda e: e.tensor_copy(out=gqk[:, 0:8, :], in_=g2[:, 0:1, :].broadcast_to([128, 8, 64])), reads=[g2B], writes=[gqkB])
        dve.op(lambda e: e.tensor_copy(out=gqk[:, 8:10, :], in_=g2[:, 1:2, :].broadcast_to([128, 2, 64])), reads=[g2B], writes=[gqkB])
        act.op(lambda e: e.activation(out=gqk[:, 0:8, :], in_=gqk[:, 0:8, :], func=AF.Copy, scale=0.125),
               reads=[gqkB], writes=[gqkB])
        r = alloc([128, 4, D], F32); rB = [Buf() for _ in range(4)]
        xn = alloc([128, 4, D], BF16); xnB = [Buf() for _ in range(4)]
        xnT = alloc([128, DC, 512], BF16); xnTB = Buf()
        junk = alloc([128, D], BF16); junkB = Buf()
        ss = alloc([128, 4], F32); rs = alloc([128, 4], F32); sB = Buf()
        rope = alloc([128, 4, 128], F32); ropeB = Buf()
        stq = alloc([128, 8, 512], BF16); stqB = Buf()
        vast = alloc([128, 4, 8, 80], BF16); vastB = Buf()
        vbst = alloc([128, 4, 2, 80], BF16); vbstB = Buf()
        qbst = alloc([128, 4, 512], BF16); qbstB = Buf()
        kbst = alloc([128, 512], BF16); kbstB = Buf()
        qk = alloc([128, 4, 10, 64], F32); qkB = Buf()
        t1 = alloc([128, 4, 10, 64], F32); t1B = Buf()
        t2 = alloc([128, 4, 10, 64], F32); t2B = Buf()
        qr = alloc([128, 4, 10, 64], BF16); qrB = Buf()
        ssq = alloc([128, 4, 10], F32); rq = alloc([128, 4, 10], F32); sqB = Buf()
        pool.op(lambda e: e.memset(vast, 1.0), writes=[vastB])
        pool.op(lambda e: e.memset(vbst, 1.0), writes=[vbstB])
        KP2 = int(os.environ.get("KP2", "9"))

        tiles = []
        for t in range(14):
            ns = 4 if t < 13 else 1
            tiles.append(("own", t * 512, ns))
        for t in range(NSF // 512):
            tiles.append(("full", t * 512, 4))
        if os.environ.get("KSMALL"):
            tiles = tiles[:15]
        if KP2 < 2:
            tiles = []
        r2 = [r, alloc([128, 4, D], F32)]; r2B = [rB, [Buf() for _ in range(4)]]
        rope2 = [rope, alloc([128, 4, 128], F32)]; rope2B = [ropeB, Buf()]

        def p2_load(tile, bi):
            kind_, t0_, ns_ = tile
            hs = dr["h1_own"] if kind_ == "own" else dr["h1_full"]
            rsrc_ = dr["rope_own"] if kind_ == "own" else dr["rope_full"]
            sp.dma(r2[bi][:, 0:ns_, :], hs[t0_:t0_ + ns_ * 128, :].rearrange("(s p) d -> p s d", p=128), writes=r2B[bi][0:ns_])
            sp.dma(rope2[bi][:, 0:ns_, :], rsrc_[t0_:t0_ + ns_ * 128, :].rearrange("(s p) d -> p s d", p=128), writes=[rope2B[bi]])
        for tix, (kind, t0, ns) in enumerate(tiles):
            N = ns * 128
            full = kind == "own"
            hsrc = dr["h1_own"] if full else dr["h1_full"]
            rsrc = dr["rope_own"] if full else dr["rope_full"]
            r = r2[tix % 2]; rB = r2B[tix % 2]; rope = rope2[tix % 2]; ropeB = rope2B[tix % 2]
            if tix == 0:
                p2_load(tiles[0], 0)
            if tix + 1 < len(tiles):
                p2_load(tiles[tix + 1], (tix + 1) % 2)
            if tix == 0:
                norm_transpose(r, rB, ns, xn, xnB, xnT, xnTB, gmix, gmixB, ss, rs, sB, junk, junkB, (0, 1))
            if full:
                for p in range(8):
                    bi = 2 + (p % 2)
                    col = (0 if p < 4 else 512) + (p % 4) * 128

                    def mmq(e, bi=bi, col=col, N=N):
                        ins = None
                        for dc in range(DC):
                            ins = e.matmul(banks[bi][:, 0:N], lhsT=win[:, dc, col:col + 128], rhs=xnT[:, dc, 0:N],
                                           start=(dc == 0), stop=(dc == DC - 1))
                        return ins
                    pe.op(mmq, reads=[winB, xnTB], writes=[bankb[bi]])
                    if p % 2 == 0:
                        act.op(lambda e, p=p, bi=bi, N=N: e.activation(out=stq[:, p, 0:N], in_=banks[bi][:, 0:N], func=AF.Copy),
                               reads=[bankb[bi]], writes=[stqB])
                    else:
                        dve.op(lambda e, p=p, bi=bi, N=N: e.tensor_copy(out=stq[:, p, 0:N], in_=banks[bi][:, 0:N]),
                               reads=[bankb[bi]], writes=[stqB])
                if t0 < NQ:
                    sp.dma(dr["qaT"][:, :, t0:t0 + N], stq[:, 0:4, 0:N], reads=[stqB])
                sp.dma(dr["kaT"][:, :, t0:t0 + N], stq[:, 4:8, 0:N], reads=[stqB])
            for s in range(ns):
                bva, bqb, bkv = (4, 5, 6) if s % 2 == 0 else (2, 3, 7)
                if full:
                    def mmv(e, s=s, bva=bva):
                        ins = None
                        for dc in range(DC):
                            ins = e.matmul(banks[bva][:], lhsT=xnT[:, dc, s * 128:(s + 1) * 128], rhs=win[:, dc, 1024:1536],
                                           start=(dc == 0), stop=(dc == DC - 1))
                        return ins
                    pe.op(mmv, reads=[winB, xnTB], writes=[bankb[bva]])
                    act.op(lambda e, s=s, bva=bva: e.activation(out=vast[:, s, :, 0:64],
                                                                in_=banks[bva][:].rearrange("p (h d) -> p h d", h=8), func=AF.Copy),
                           reads=[bankb[bva]], writes=[vastB])

                    def mmqb(e, s=s, bqb=bqb):
                        ins = None
                        for dc in range(DC):
                            ins = e.matmul(banks[bqb][:], lhsT=xnT[:, dc, s * 128:(s + 1) * 128], rhs=win[:, dc, 1536:2048],
                                           start=(dc == 0), stop=(dc == DC - 1))
                        return ins
                    pe.op(mmqb, reads=[winB, xnTB], writes=[bankb[bqb]])
                    act.op(lambda e, s=s, bqb=bqb: e.activation(out=qk[:, s, 0:8, :], in_=banks[bqb][:].rearrange("p (h d) -> p h d", h=8),
                                                                func=AF.Copy), reads=[bankb[bqb]], writes=[qkB])

                def mmkv(e, s=s, bkv=bkv):
                    ins = None
                    for dc in range(DC):
                        ins = e.matmul(banks[bkv][:, 0:256], lhsT=xnT[:, dc, s * 128:(s + 1) * 128], rhs=win[:, dc, 2048:2304],
                                       start=(dc == 0), stop=(dc == DC - 1))
                    return ins
                pe.op(mmkv, reads=[winB, xnTB], writes=[bankb[bkv]])
                dve.op(lambda e, s=s, bkv=bkv: e.tensor_copy(out=qk[:, s, 8:10, :], in_=banks[bkv][:, 0:128].rearrange("p (h d) -> p h d", h=2)),
                       reads=[bankb[bkv]], writes=[qkB])
                dve.op(lambda e, s=s, bkv=bkv: e.tensor_copy(out=vbst[:, s, :, 0:64],
                                                             in_=banks[bkv][:, 128:256].rearrange("p (h d) -> p h d", h=2)),
                       reads=[bankb[bkv]], writes=[vbstB])
            if tix + 1 < len(tiles):
                nb = (tix + 1) % 2
                norm_transpose(r2[nb], r2B[nb], tiles[tix + 1][2], xn, xnB, xnT, xnTB, gmix, gmixB, ss, rs, sB, junk, junkB, (0, 1))
            h0 = 0 if full else 8
            nh = 10 - h0
            Q = qk[:, 0:ns, h0:10, :]
            T1 = t1[:, 0:ns, h0:10, :]
            dve.op(lambda e, Q=Q, T1=T1: e.tensor_tensor(out=T1, in0=Q, in1=Q, op=ALU.mult), reads=[qkB], writes=[t1B])
            dve.op(lambda e, T1=T1, h0=h0, ns=ns: e.tensor_reduce(out=ssq[:, 0:ns, h0:10], in_=T1, axis=AX.X, op=ALU.add),
                   reads=[t1B], writes=[sqB])
            act.op(lambda e, h0=h0, ns=ns: e.activation(out=rq[:, 0:ns, h0:10], in_=ssq[:, 0:ns, h0:10], func=AF.Ln, scale=1.0 / 64, bias=EPS),
                   reads=[sqB], writes=[sqB])
            act.op(lambda e, h0=h0, ns=ns: e.activation(out=rq[:, 0:ns, h0:10], in_=rq[:, 0:ns, h0:10], func=AF.Exp, scale=-0.5),
                   reads=[sqB], writes=[sqB])
            for s in range(ns):
                dve.op(lambda e, s=s, h0=h0, nh=nh: e.tensor_tensor(out=t1[:, s, h0:10, :], in0=qk[:, s, h0:10, :],
                                                                   in1=rq[:, s, h0:10].unsqueeze(2).broadcast_to([128, nh, 64]), op=ALU.mult),
                       reads=[qkB, sqB, t1B], writes=[t1B])
            dve.op(lambda e, Q=Q, T1=T1, h0=h0, ns=ns, nh=nh: e.tensor_tensor(
                out=Q, in0=T1, in1=gqk[:, h0:10, :].unsqueeze(1).broadcast_to([128, ns, nh, 64]), op=ALU.mult),
                reads=[t1B, gqkB], writes=[qkB])
            Cb = rope[:, 0:ns, 0:64].unsqueeze(2).broadcast_to([128, ns, nh, 64])
            dve.op(lambda e, Q=Q, T1=T1, Cb=Cb: e.tensor_tensor(out=T1, in0=Q, in1=Cb, op=ALU.mult),
                   reads=[qkB, ropeB], writes=[t1B])
            for hf in range(2):
                for xi in range(2):
                    lo = hf * 32 + xi * 16
                    so = hf * 32 + (1 - xi) * 16
                    Sb = rope[:, 0:ns, 64 + lo:64 + lo + 16].unsqueeze(2).broadcast_to([128, ns, nh, 16])
                    dve.op(lambda e, h0=h0, lo=lo, so=so, Sb=Sb, ns=ns: e.tensor_tensor(
                        out=t2[:, 0:ns, h0:10, lo:lo + 16], in0=qk[:, 0:ns, h0:10, so:so + 16], in1=Sb, op=ALU.mult),
                        reads=[qkB, ropeB], writes=[t2B])
            dve.op(lambda e, h0=h0, ns=ns, T1=T1: e.tensor_tensor(out=qr[:, 0:ns, h0:10, :], in0=T1, in1=t2[:, 0:ns, h0:10, :], op=ALU.add),
                   reads=[t1B, t2B], writes=[qrB])
            for s in range(ns):
                qrf = qr[:, s].rearrange("p h d -> p (h d)")
                pbi = 7 if s % 2 == 0 else 4
                ptb = banks[pbi][:].bitcast(BF16)
                if full:
                    def trq(e, qrf=qrf, ptb=ptb):
                        ins = None
                        for j in range(4):
                            ins = e.transpose(ptb[:, j * 128:(j + 1) * 128], qrf[:, j * 128:(j + 1) * 128], ident_b)
                        ins = e.transpose(ptb[:, 512:640], qrf[:, 512:640], ident_b)
                        return ins
                    pe.op(trq, reads=[qrB, identB], writes=[bankb[pbi]])
                    act.op(lambda e, s=s, ptb=ptb: e.activation(out=qbst[:, :, s * 128:(s + 1) * 128],
                                                                in_=ptb[:, 0:512].rearrange("p (j t) -> p j t", j=4), func=AF.Copy),
                           reads=[bankb[pbi]], writes=[qbstB])
                else:
                    pe.op(lambda e, qrf=qrf, ptb=ptb: e.transpose(ptb[:, 512:640], qrf[:, 512:640], ident_b),
                          reads=[qrB, identB], writes=[bankb[pbi]])
                act.op(lambda e, s=s, ptb=ptb: e.activation(out=kbst[:, s * 128:(s + 1) * 128], in_=ptb[:, 512:640], func=AF.Copy),
                       reads=[bankb[pbi]], writes=[kbstB])
            if KP2 < 7:
                continue
            KST = int(os.environ.get("KST", "63"))
            if full:
                if KST & 1:
                    sp.dma(dr["va"][t0:t0 + N, :].rearrange("(s p) c -> p s c", p=128),
                           vast[:, 0:ns].rearrange("p s h d -> p s (h d)"), reads=[vastB])
                if t0 < NQ and (KST & 2):
                    sp.dma(dr["qbT"][:, :, t0:t0 + N], qbst[:, :, 0:N], reads=[qbstB])
                if KST & 4:
                    KBV = int(os.environ.get("KBV", "0"))
                    if KBV == 1:
                        pool.dma(dr["kbT_own"][:, t0:t0 + N], kbst[:, 0:N], reads=[kbstB])
                    elif KBV == 2:
                        if N == 512:
                            sp.dma(dr["kbT_own"][:, t0:t0 + N], kbst[:, 0:N], reads=[kbstB])
                    elif KBV == 4:
                        sp.dma(dr["kbT_own"][:, t0:t0 + N], kbst[:, 0:N], reads=[])
                    elif KBV == 5:
                        if t0 != 6656:
                            sp.dma(dr["kbT_own"][:, t0:t0 + N], kbst[:, 0:N], reads=[kbstB])
                    elif KBV == 3:
                        sp.dma(dr["kbT_own"][:, t0:t0 + N].unsqueeze(1), kbst[:, 0:N].unsqueeze(1), reads=[kbstB])
                    else:
                        sp.dma(dr["kbT_own"][:, t0:t0 + N], kbst[:, 0:N], reads=[kbstB])
                if KST & 8:
                    sp.dma(dr["vb_own"][t0:t0 + N, :].rearrange("(s p) c -> p s c", p=128),
                           vbst[:, 0:ns].rearrange("p s h d -> p s (h d)"), reads=[vbstB])
            else:
                if KST & 16:
                    sp.dma(dr["kbT_full"][:, t0:t0 + N], kbst[:, 0:N], reads=[kbstB])
                if KST & 32:
                    sp.dma(dr["vb_full"][t0:t0 + N, :].rearrange("(s p) c -> p s c", p=128),
                           vbst[:, 0:ns].rearrange("p s h d -> p s (h d)"), reads=[vbstB])
        barrier()

    def attn_phase():
        st["off"] = base_off
        KTM = 129
        kT = alloc([128, 128 * 128 + 16], BF16); kTB = Buf()
        vS = alloc([128, KTM, 160], BF16); vSB = Buf()
        kaM = alloc([128, 4, 16], BF16); vaM = alloc([16, 640], BF16); mbT = alloc([16, 8], F32); metaB = Buf()
        sp.dma(kaM, dr["kaT"][:, :, NQ:NQ + 16], writes=[metaB])
        sp.dma(vaM, dr["va"][NQ:NQ + 16, :], writes=[metaB])
        mb8 = alloc([8, 16], F32); mb8B = Buf()
        sp.dma(mb8, dr["meta_bias"], writes=[mb8B])
        pe.op(lambda e: e.transpose(banks[0][0:16, 0:8], mb8, ident_f[0:8, 0:8]), reads=[mb8B, identB], writes=[bankb[0]])
        act.op(lambda e: e.activation(out=mbT, in_=banks[0][0:16, 0:8], func=AF.Copy), reads=[bankb[0]], writes=[metaB])
        qa2 = [alloc([128, 4, 256], BF16) for _ in range(2)]; qa2B = [Buf(), Buf()]
        kaw2 = [alloc([128, 4, 768], BF16) for _ in range(2)]; kaw2B = [Buf(), Buf()]
        vaw2 = [alloc([128, 6, 640], BF16) for _ in range(2)]; vaw2B = [Buf(), Buf()]
        qb2 = [alloc([128, 4, 256], BF16) for _ in range(2)]; qb2B = [Buf(), Buf()]
        tab = [alloc([128, 6, 2, 256], F32) for _ in range(2)]; tabB = [Buf(), Buf()]
        tmp = [alloc([128, 512], F32) for _ in range(4)]; tmpB = [Buf() for _ in range(4)]
        pn = [alloc([128, 512], BF16) for _ in range(4)]; pnB = [Buf() for _ in range(4)]
        pg2 = [alloc([128, 1024], BF16) for _ in range(3)]; pg2B = [Buf() for _ in range(3)]
        oa = alloc([65, 8, 256], F32); oaB = Buf()
        ob = alloc([65, 2, 2, 512], F32); obB = Buf()
        yc = alloc([128, 16, 64], F32); ycB = Buf()
        rc = alloc([128, 16], F32); rcB = Buf()
        yo = alloc([128, D], BF16); yoB = Buf()
        junk = alloc([128, 512], BF16); junkB = Buf()
        ss = alloc([128, 2], F32); rs = alloc([128, 2], F32); sB = Buf()
        cnt = {"s": 0, "g": 0, "blk": 0}

        def run_seq(segs, blocks):
            nkt = sum(n for (_k, _v, n) in segs) // 128
            NK = nkt * 128
            base = 0
            for (kt_src, v_src, n) in segs:
                for c in range(0, n, 2048):
                    sp.dma(kT[:, base + c:base + c + 2048], kt_src[:, c:c + 2048], writes=[kTB])
                for c in range(0, n // 128, 16):
                    sp.dma(vS[:, base // 128 + c:base // 128 + c + 16, :],
                           v_src[c * 128:(c + 16) * 128, :].rearrange("(t p) c -> p t c", p=128), writes=[vSB])
                base += n
            sp.dma(kT[:, NK:NK + 16], dr["kbT_own"][:, NQ:NQ + 16], writes=[kTB])
            sp.dma(vS[0:16, nkt, :], dr["vb_own"][NQ:NQ + 16, :], writes=[vSB])
            def blk_load(blk, bi):
                q0_, w0_, _t = blk
                sp.dma(qa2[bi], dr["qaT"][:, :, q0_:q0_ + 256], writes=[qa2B[bi]])
                sp.dma(kaw2[bi], dr["kaT"][:, :, w0_:w0_ + 768], writes=[kaw2B[bi]])
                sp.dma(vaw2[bi], dr["va"][w0_:w0_ + 768, :].rearrange("(t p) c -> p t c", p=128), writes=[vaw2B[bi]])
                sp.dma(qb2[bi], dr["qbT"][:, :, q0_:q0_ + 256], writes=[qb2B[bi]])
            def do_block(q0, w0, typ, qa, qaB, kaw, kawB, vaw, vawB, qb, qbB):
                steps = [(p, o) for p in range(4) for o in range(7)]

                def na_front(i):
                    p, o = steps[i]
                    tb = tab[p % 2]; tbB = tabB[p % 2]
                    if o == 0:
                        for e2 in range(2):
                            sp.dma(tb[:, :, e2, :], dr["na_tab"][typ, 2 * p + e2].rearrange("k (o q) -> k o q", o=6), writes=[tbB])
                    j = cnt["g"] % 2; cnt["g"] += 1
                    k = i % 4
                    sb2 = [bankb[2 * j], bankb[2 * j + 1]]
                    if o < 6:
                        def qk(e):
                            ins = None
                            for e2 in range(2):
                                ins = e.matmul(banks[2 * j + e2][:, 0:256],
                                               lhsT=kaw[64 * e2:64 * e2 + 64, p, o * 128:(o + 1) * 128],
                                               rhs=qa[64 * e2:64 * e2 + 64, p, :], start=True, stop=True)
                            return ins
                        pe.op(qk, reads=[kawB, qaB], writes=sb2)
                        dve.op(lambda e: e.scalar_tensor_tensor(
                            out=tmp[k].rearrange("k (e q) -> k e q", e=2),
                            in0=big[j][:, :].rearrange("k (e q) -> k e q", e=2)[:, :, 0:256], scalar=0.125, in1=tb[:, o],
                            op0=ALU.mult, op1=ALU.add), reads=sb2 + [tbB], writes=[tmpB[k]])
                        act.op(lambda e: e.activation(out=pn[k], in_=tmp[k], func=AF.Exp),
                               reads=[tmpB[k]], writes=[pnB[k]])
                    else:
                        def qkm(e):
                            ins = None
                            for e2 in range(2):
                                ins = e.matmul(banks[2 * j + e2][0:16, 0:256], lhsT=kaM[64 * e2:64 * e2 + 64, p, :],
                                               rhs=qa[64 * e2:64 * e2 + 64, p, :], start=True, stop=True)
                            return ins
                        pe.op(qkm, reads=[metaB, qaB], writes=sb2)
                        for e2 in range(2):
                            h = 2 * p + e2
                            act.op(lambda e, e2=e2, h=h: e.activation(
                                out=pn[k][0:16, e2 * 256:(e2 + 1) * 256], in_=banks[2 * j + e2][0:16, 0:256],
                                func=AF.Exp, scale=0.125, bias=mbT[:, h:h + 1]), reads=[bankb[2 * j + e2], metaB], writes=[pnB[k]])

                def na_back(i):
                    p, o = steps[i]
                    k = i % 4

                    def pv(e):
                        ins = None
                        for e2 in range(2):
                            h = 2 * p + e2
                            if o < 6:
                                ins = e.matmul(banks[4 + e2][0:65, 0:256], lhsT=vaw[:, o, h * 80:h * 80 + 65],
                                               rhs=pn[k][:, e2 * 256:(e2 + 1) * 256], start=(o == 0), stop=False)
                            else:
                                ins = e.matmul(banks[4 + e2][0:65, 0:256], lhsT=vaM[:, h * 80:h * 80 + 65],
                                               rhs=pn[k][0:16, e2 * 256:(e2 + 1) * 256], start=False, stop=True)
                        return ins
                    pe.op(pv, reads=[vawB, metaB, pnB[k]], writes=[bankb[4], bankb[5]])
                    if o == 6:
                        act.op(lambda e: e.activation(out=oa[:, 2 * p, :], in_=banks[4][0:65, 0:256], func=AF.Copy),
                               reads=[bankb[4]], writes=[oaB])
                        dve.op(lambda e: e.tensor_copy(out=oa[:, 2 * p + 1, :], in_=banks[5][0:65, 0:256]),
                               reads=[bankb[5]], writes=[oaB])
                LN = 3
                for t in range(len(steps) + LN):
                    if t < len(steps):
                        na_front(t)
                    if t >= LN:
                        na_back(t - LN)
                units = [(qt, kt) for qt in range(2) for kt in range(nkt + 1)]

                def g_front(i):
                    qt, kt = units[i]
                    j = cnt["g"] % 2; cnt["g"] += 1
                    k = i % 3
                    kk = 128 if kt < nkt else 16

                    def qk(e):
                        ins = None
                        for g in range(2):
                            ins = e.matmul(banks[2 * j + g][0:kk, :], lhsT=kT[64 * g:64 * g + 64, kt * 128:kt * 128 + kk],
                                           rhs=qb[64 * g:64 * g + 64, :, qt * 128:(qt + 1) * 128], start=True, stop=True)
                        return ins
                    pe.op(qk, reads=[kTB, qbB], writes=[bankb[2 * j], bankb[2 * j + 1]])
                    act.op(lambda e: e.activation(out=pg2[k][0:kk, :], in_=big[j][0:kk, :], func=AF.Exp),
                           reads=[bankb[2 * j], bankb[2 * j + 1]], writes=[pg2B[k]])

                def g_back(i):
                    qt, kt = units[i]
                    k = i % 3
                    kk = 128 if kt < nkt else 16

                    def pv(e):
                        ins = None
                        for g in range(2):
                            ins = e.matmul(banks[4 + g][0:65, :], lhsT=vS[0:kk, kt, g * 80:g * 80 + 65],
                                           rhs=pg2[k][0:kk, g * 512:(g + 1) * 512],
                                           start=(kt == 0), stop=(kt == nkt))
                        return ins
                    pe.op(pv, reads=[vSB, pg2B[k]], writes=[bankb[4], bankb[5]])
                    if kt == nkt:
                        dve.op(lambda e: e.tensor_copy(out=ob[:, 0, qt, :], in_=banks[4][0:65, :]),
                               reads=[bankb[4]], writes=[obB])
                        act.op(lambda e: e.activation(out=ob[:, 1, qt, :], in_=banks[5][0:65, :], func=AF.Copy),
                               reads=[bankb[5]], writes=[obB])
                LG = 1
                for t in range(len(units) + LG):
                    if t < len(units):
                        g_front(t)
                    if t >= LG:
                        g_back(t - LG)
                for qt in range(2):
                    for grp in range(4):
                        bi = 6 + (grp % 2)
                        pt = banks[bi][:, 0:260].rearrange("p (h c) -> p h c", h=4)

                        def trf(e, grp=grp, qt=qt, pt=pt):
                            ins = None
                            for j in range(4):
                                if grp < 2:
                                    src = oa[:, grp * 4 + j, qt * 128:(qt + 1) * 128]
                                else:
                                    src = ob[:, grp - 2, qt, j * 128:(j + 1) * 128]
                                ins = e.transpose(pt[:, j, :], src, ident_f[0:65, 0:65])
                            return ins
                        pe.op(trf, reads=[oaB, obB, identB], writes=[bankb[bi]])
                        dve.op(lambda e, grp=grp, pt=pt: e.reciprocal(out=rc[:, grp * 4:(grp + 1) * 4], in_=pt[:, :, 64]),
                               reads=[bankb[bi]], writes=[rcB])
                        dve.op(lambda e, grp=grp, pt=pt: e.tensor_tensor(
                            out=yc[:, grp * 4:(grp + 1) * 4, :], in0=pt[:, :, 0:64],
                            in1=rc[:, grp * 4:(grp + 1) * 4].unsqueeze(2).broadcast_to([128, 4, 64]), op=ALU.mult),
                            reads=[bankb[bi], rcB], writes=[ycB])
                    ycf = yc.rearrange("p h d -> p (h d)")
                    for a in range(2):
                        act.op(lambda e, a=a, ycf=ycf: e.activation(out=junk, in_=ycf[:, a * 512:(a + 1) * 512], func=AF.Square,
                                                                    accum_out=ss[:, a:a + 1]), reads=[ycB], writes=[junkB, sB])
                    rstd_ops(ss, rs, 2, 1.0 / 512, [sB], [sB])
                    for a in range(2):
                        dve.op(lambda e, a=a, ycf=ycf: e.tensor_scalar(out=yo[:, a * 512:(a + 1) * 512], in0=ycf[:, a * 512:(a + 1) * 512],
                                                                       scalar1=rs[:, a:a + 1], scalar2=None, op0=ALU.mult),
                               reads=[ycB, sB], writes=[yoB])
                    sp.dma(dr["ycat"][q0 + qt * 128:q0 + (qt + 1) * 128, :], yo, reads=[yoB])


            for bix, blk in enumerate(blocks):
                if bix == 0:
                    blk_load(blocks[0], cnt["blk"] % 2)
                bi = cnt["blk"] % 2; cnt["blk"] += 1
                if bix + 1 < len(blocks):
                    blk_load(blocks[bix + 1], cnt["blk"] % 2)
                do_block(blk[0], blk[1], blk[2], qa2[bi], qa2B[bi], kaw2[bi], kaw2B[bi],
                         vaw2[bi], vaw2B[bi], qb2[bi], qb2B[bi])

        blocks = []
        for b in range(16):
            R = 4 * b
            Rw = min(max(R - 4, 0), 52)
            typ = 1 if b == 0 else (2 if b == 15 else 0)
            blocks.append((R * 64, Rw * 64, typ))
        if os.environ.get("KSMALL"):
            blocks = blocks[:2]
        seg_p = [(dr["kbT_own"][:, 0:NPR], dr["vb_own"][0:NPR, :], NPR)]
        run_seq(seg_p, blocks)
        blocks = []
        for b in range(8):
            Rl = 4 + 4 * b
            typ = 3 if b == 0 else (4 if b == 7 else 0)
            blocks.append((NPR + Rl * 64, NPR + (Rl - 4) * 64, typ))
        o0 = NPR + 256
        seg_s = [(dr["kbT_full"], dr["vb_full"], NSF),
                 (dr["kbT_own"][:, o0:o0 + 2048], dr["vb_own"][o0:o0 + 2048, :], 2048)]
        if os.environ.get("KSMALL"):
            run_seq(seg_p, blocks[:1])
        else:
            run_seq(seg_s, blocks)
        barrier()

    def wout_phase(qtiles):
        st["off"] = base_off
        wo = alloc([128, DC, D], BF16); woB = Buf()
        load_weight(wo, dr["w_out"], DC, woB)
        ggrp, ggrpB = colgain("grp_ab")
        gpost, gpostB = rowgain("mix_post")
        r = alloc([128, 4, D], F32); rB = [Buf() for _ in range(4)]
        yt = alloc([128, 4, D], BF16); ytB = [Buf() for _ in range(4)]
        ycT = alloc([128, DC, 512], BF16); ycTB = Buf()
        ysb = alloc([128, D], F32); ysbB = Buf()
        junk = alloc([128, D], BF16); junkB = Buf()
        ss = alloc([128, 4], F32); rs = alloc([128, 4], F32); sB = Buf()
        rr = [r, alloc([128, 4, D], F32)]; rrB = [rB, [Buf() for _ in range(4)]]
        yy = [yt, alloc([128, 4, D], BF16)]; yyB = [ytB, [Buf() for _ in range(4)]]

        def wo_load(t0_, bi):
            sp.dma(rr[bi], dr["h1_own"][t0_:t0_ + 512, :].rearrange("(s p) d -> p s d", p=128), writes=rrB[bi])
            sp.dma(yy[bi], dr["ycat"][t0_:t0_ + 512, :].rearrange("(s p) d -> p s d", p=128), writes=yyB[bi])
        for tix, t0 in enumerate(qtiles):
            if tix == 0:
                wo_load(qtiles[0], 0)
            if tix + 1 < len(qtiles):
                wo_load(qtiles[tix + 1], (tix + 1) % 2)
            r, rB, yt, ytB = rr[tix % 2], rrB[tix % 2], yy[tix % 2], yyB[tix % 2]
            for dc in range(DC):
                bi = dc % 2
                pt = banks[bi][:].bitcast(BF16)

                def tr(e, dc=dc, pt=pt, yt=yt):
                    ins = None
                    for s in range(4):
                        ins = e.transpose(pt[:, s * 128:(s + 1) * 128], yt[:, s, dc * 128:(dc + 1) * 128], ident_b)
                    return ins
                pe.op(tr, reads=ytB + [identB], writes=[bankb[bi]])
                if dc % 2 == 0:
                    act.op(lambda e, dc=dc, pt=pt: e.activation(out=ycT[:, dc, :], in_=pt[:, 0:512], func=AF.Copy,
                                                                scale=ggrp[:, dc:dc + 1]), reads=[bankb[bi], ggrpB], writes=[ycTB])
                else:
                    dve.op(lambda e, dc=dc, pt=pt: e.tensor_scalar(out=ycT[:, dc, :], in0=pt[:, 0:512], scalar1=ggrp[:, dc:dc + 1],
                                                                   scalar2=None, op0=ALU.mult), reads=[bankb[bi], ggrpB], writes=[ycTB])
            for s in range(4):
                def mo(e, s=s):
                    ins = None
                    for hf in range(2):
                        for dc in range(DC):
                            ins = e.matmul(banks[6 + hf][:], lhsT=ycT[:, dc, s * 128:(s + 1) * 128],
                                           rhs=wo[:, dc, hf * 512:(hf + 1) * 512], start=(dc == 0), stop=(dc == DC - 1))
                    return ins
                pe.op(mo, reads=[ycTB, woB], writes=[bankb[6], bankb[7]])
                post_resid((6, 7), r, rB, s, ysb, ysbB, junk, junkB, ss, rs, sB, gpost, gpostB, 1.0)
            sp.dma(dr["h2"][t0:t0 + 512, :].rearrange("(s p) d -> p s d", p=128), r, reads=rB)
        barrier()

    t1 = []
    for t in range(14):
        ns = 4 if t < 13 else 1
        t1.append((dr["x_own"][t * 512:t * 512 + ns * 128, :], dr["h1_own"][t * 512:t * 512 + ns * 128, :], ns))
    for t in range(NSF // 512):
        t1.append((dr["xs_full"][t * 512:(t + 1) * 512, :], dr["h1_full"][t * 512:(t + 1) * 512, :], 4))
    STOP = int(os.environ.get("KSTOP", "9"))
    if os.environ.get("KSMALL"):
        t1 = t1[:15]
    ffn_phase(t1, "ffn1", False)
    if STOP >= 2:
        proj_phase()
    if STOP >= 3:
        attn_phase()
    qtiles = [t * 512 for t in range(8)] + [NPR + 256 + t * 512 for t in range(4)]
    if os.environ.get("KSMALL"):
        qtiles = qtiles[:1]
    if STOP >= 4:
        wout_phase(qtiles)
    t4 = []
    for t in range(8):
        t4.append((dr["h2"][t * 512:(t + 1) * 512, :], y_p[t * 512:(t + 1) * 512, :], 4))
    for t in range(4):
        a = NPR + 256 + t * 512
        t4.append((dr["h2"][a:a + 512, :], y_s[t * 512:(t + 1) * 512, :], 4))
    if os.environ.get("KSMALL"):
        t4 = t4[:1]
    if STOP >= 5:
        ffn_phase(t4, "ffn2", True)

    with nc.allow_non_contiguous_dma(reason="small strided constant loads"):
        with nc.Block() as block:
            @block.tensor
            def _(e):
                for f in pe.prog:
                    f(e)

            @block.scalar
            def _(e):
                for f in act.prog:
                    f(e)

            @block.vector
            def _(e):
                for f in dve.prog:
                    f(e)

            @block.gpsimd
            def _(e):
                for f in pool.prog:
                    f(e)

            @block.sync
            def _(e):
                for f in sp.prog:
                    f(e)
    es.close()
    return nc


_CACHE = {}


def kernel(**inputs):
    f32 = lambda a: np.ascontiguousarray(np.asarray(a, dtype=np.float32))
    xp = f32(inputs["x_prompt"]); xs = f32(inputs["x_sample"])[0]
    meta = f32(inputs["meta_tokens"])
    rel_bias = f32(inputs["na_rel_bias"])[0]
    shared = {
        "w_in": f32(inputs["w_in"])[0], "w_out": f32(inputs["w_out"])[0],
        "mix_pre": f32(inputs["mix_norm_pre"])[0], "mix_post": f32(inputs["mix_norm_post"])[0],
        "final_norm": f32(inputs["final_norm"]),
        "q_norm": f32(inputs["gqa_q_norm"])[0], "k_norm": f32(inputs["gqa_k_norm"])[0],
        "grp_ab": np.concatenate([f32(inputs["grp_norm_a"])[0], f32(inputs["grp_norm_b"])[0]]),
        "meta_bias": f32(inputs["na_meta_bias"])[0],
    }
    for f in ("ffn1", "ffn2"):
        shared[f + "_wg"] = f32(inputs[f + "_w_gate"])[0]
        shared[f + "_wu"] = f32(inputs[f + "_w_up"])[0]
        shared[f + "_wd"] = f32(inputs[f + "_w_down"])[0]
        shared[f + "_pre"] = f32(inputs[f + "_norm_pre"])[0]
        shared[f + "_post"] = f32(inputs[f + "_norm_post"])[0]
    tall = np.arange(16384)
    rope_all = _rope_table(tall // 64, tall % 64)
    in_maps = []
    for c in range(8):
        x_own = np.zeros((NOWN, D), np.float32)
        x_own[0:NPR] = xp[c]
        g0 = 32 * c - 4
        lo, hi = max(g0, 0), min(g0 + 40, 256)
        x_own[NPR + (lo - g0) * 64:NPR + (hi - g0) * 64] = xs[lo * 64:hi * 64]
        x_own[NQ:NQ + 16] = meta
        prow = np.zeros(NOWN, np.float32); pcol = np.zeros(NOWN, np.float32)
        tp = np.arange(NPR)
        prow[0:NPR] = tp // 64; pcol[0:NPR] = tp % 64
        ts = np.arange(NSC)
        prow[NPR:NQ] = g0 + ts // 64; pcol[NPR:NQ] = ts % 64
        prow[NQ:NQ + 16] = -1.0; pcol[NQ:NQ + 16] = np.arange(16)
        m = dict(shared)
        keep = np.ones(16384, bool)
        keep[2048 * c:2048 * (c + 1)] = False
        m["xs_full"] = np.ascontiguousarray(xs[keep])
        m["rope_full"] = np.ascontiguousarray(rope_all[keep])
        m["x_own"] = x_own
        m["rope_own"] = _rope_table(prow, pcol)
        m["na_tab"] = _na_tables(rel_bias, c)
        in_maps.append(m)
    if "nc" not in _CACHE:
        _CACHE["nc"] = build_program()
    res = run_bass_kernel_spmd(_CACHE["nc"], in_maps, core_ids=list(range(8)))
    y_prompt = np.stack([np.asarray(res.results[c]["y_p"], np.float32) for c in range(8)], 0)
    y_sample = np.concatenate([np.asarray(res.results[c]["y_s"], np.float32) for c in range(8)], 0)[None]
    return (y_prompt, y_sample)
```

```python
import os
import numpy as np
import ml_dtypes
from contextlib import ExitStack
import concourse.bass as bass
import concourse.mybir as mybir
from concourse.bass_utils import run_bass_kernel_spmd

F32 = mybir.dt.float32
BF16 = mybir.dt.bfloat16
AF = mybir.ActivationFunctionType
ALU = mybir.AluOpType
AX = mybir.AxisListType

D = 1024
DC = 8
FF = 2816
FC = 22
NPR = 4096
NSC = 2560
NQ = NPR + NSC
NOWN = NQ + 128
NSF = 16384 - 2048
EPS = 1e-6
NEG = -30000.0
ARENA = 103000


class Buf:
    __slots__ = ("w", "rs")

    def __init__(self):
        self.w = None
        self.rs = []


class Eng:
    def __init__(self, name, sem, is_pe=False):
        self.name = name
        self.sem = sem
        self.n = 0
        self.seen = {}
        self.prog = []
        self.is_pe = is_pe
        self.pool = None
        self.pool_i = 0

    def _deps(self, reads, writes):
        toks = []
        for b in reads:
            if b.w is not None:
                toks.append(b.w)
        for b in writes:
            if b.w is not None:
                toks.append(b.w)
            toks.extend(b.rs)
        waits = []
        for (s, v) in toks:
            if self.is_pe and s is self.sem:
                continue
            if self.seen.get(id(s), 0) >= v:
                continue
            self.seen[id(s)] = v
            waits.append((s, v))
        return waits

    @staticmethod
    def _mark(tok, reads, writes):
        for b in reads:
            b.rs.append(tok)
        for b in writes:
            b.w = tok
            b.rs = []

    def op(self, fn, reads=(), writes=()):
        waits = self._deps(reads, writes)
        self.n += 1
        sem = self.sem

        def emit(e, waits=waits, fn=fn, sem=sem):
            for (s, v) in waits:
                e.wait_ge(s, v)
            fn(e).then_inc(sem, 1)
        self.prog.append(emit)
        tok = (sem, self.n)
        self._mark(tok, reads, writes)
        return tok

    def dma(self, out, in_, reads=(), writes=(), slow=False):
        waits = self._deps(reads, writes)
        ent = self.pool[self.pool_i % len(self.pool)]
        self.pool_i += 1
        sem, prev = ent
        if prev > 0 and self.seen.get(id(sem), 0) < prev:
            self.seen[id(sem)] = prev
            waits.append((sem, prev))
        ent[1] = prev + 16

        def emit(e, waits=waits, out=out, in_=in_, sem=sem, slow=slow):
            for (s, v) in waits:
                e.wait_ge(s, v)
            if slow:
                e.dma_start(out=out, in_=in_, allow_slow_non_contiguous=True).then_inc(sem, 16)
            else:
                e.dma_start(out=out, in_=in_).then_inc(sem, 16)
        self.prog.append(emit)
        tok = (sem, prev + 16)
        self._mark(tok, reads, writes)
        return tok


class K:
    pass


def _na_tables(rel_bias, core):
    rb = np.asarray(rel_bias, np.float32)

    def build(qrows, krow0, rows, kvalid_fn):
        tab = np.full((8, 128, 6, 256), NEG, np.float32)
        kc = np.arange(64)
        qc = np.arange(64)
        c0 = np.clip(qc - 8, 0, 48)
        colok = (kc[:, None] >= c0[None, :]) & (kc[:, None] < c0[None, :] + 16)
        dc = np.clip(kc[:, None] - qc[None, :] + 15, 0, 30)
        for o in range(6):
            for krl in range(2):
                kr = krow0 + 2 * o + krl
                if not kvalid_fn(kr):
                    continue
                for qi, qr in enumerate(qrows):
                    r0 = min(max(qr - 4, 0), rows - 8)
                    if not (r0 <= kr < r0 + 8):
                        continue
                    dr = kr - qr + 7
                    vals = rb[:, dr, :][:, dc]
                    vals = np.where(colok[None], vals, np.float32(NEG))
                    tab[:, krl * 64:(krl + 1) * 64, o, qi * 64:(qi + 1) * 64] = vals
        return tab

    t_int = build([8, 9, 10, 11], 4, 64, lambda r: True)
    t_pf = build([0, 1, 2, 3], 0, 64, lambda r: True)
    t_pl = build([60, 61, 62, 63], 52, 64, lambda r: True)
    g0 = 32 * core - 4
    okr = lambda r: 0 <= r < 256
    t_sf = build([g0 + 4 + i for i in range(4)], g0, 256, okr)
    t_sl = build([g0 + 32 + i for i in range(4)], g0 + 28, 256, okr)
    return np.stack([t_int, t_pf, t_pl, t_sf, t_sl]).reshape(5, 8, 128, 1536)


def _rope_table(pos_row, pos_col):
    inv = (10000.0 ** (-np.arange(0, 32, 2) / 32)).astype(np.float32)
    ar = pos_row.astype(np.float32)[:, None] * inv[None, :]
    ac = pos_col.astype(np.float32)[:, None] * inv[None, :]
    cr, sr, cc, sc = np.cos(ar), np.sin(ar), np.cos(ac), np.sin(ac)
    C = np.concatenate([cr, cr, cc, cc], 1)
    S = np.concatenate([-sr, sr, -sc, sc], 1)
    return np.concatenate([C, S], 1).astype(np.float32)


def build_program():
    nc = bass.Bass("TRN2", target_bir_lowering=False)
    k = K()
    dr = {}

    def din(name, shape, dt=F32):
        dr[name] = nc.dram_tensor(name, list(shape), dt, kind="ExternalInput").ap()

    def dscr(name, shape, dt):
        dr[name] = nc.dram_tensor(name, list(shape), dt).ap()

    din("x_own", [NOWN, D]); din("xs_full", [NSF, D])
    din("rope_own", [NOWN, 128]); din("rope_full", [NSF, 128])
    din("na_tab", [5, 8, 128, 1536])
    for f in ("ffn1", "ffn2"):
        din(f + "_wg", [D, FF]); din(f + "_wu", [D, FF]); din(f + "_wd", [FF, D])
        din(f + "_pre", [D]); din(f + "_post", [D])
    din("w_in", [D, 2304]); din("w_out", [D, D])
    din("mix_pre", [D]); din("mix_post", [D]); din("final_norm", [D])
    din("q_norm", [64]); din("k_norm", [64]); din("grp_ab", [D]); din("meta_bias", [8, 16])
    y_p = nc.dram_tensor("y_p", [NPR, D], F32, kind="ExternalOutput").ap()
    y_s = nc.dram_tensor("y_s", [2048, D], F32, kind="ExternalOutput").ap()
    dscr("h1_own", [NOWN, D], F32); dscr("h1_full", [NSF, D], F32)
    dscr("qaT", [128, 4, NQ], BF16); dscr("kaT", [128, 4, NOWN], BF16)
    dscr("va", [NOWN, 640], BF16)
    dscr("qbT", [128, 4, NQ], BF16); dscr("kbT_own", [128, NOWN], BF16)
    dscr("vb_own", [NOWN, 160], BF16)
    dscr("kbT_full", [128, NSF], BF16); dscr("vb_full", [NSF, 160], BF16)
    dscr("ycat", [NQ, D], BF16); dscr("h2", [NQ, D], F32)

    es = ExitStack()
    arena = es.enter_context(nc.sbuf_tensor("arena", [128, ARENA], BF16))
    big = [es.enter_context(nc.psum_tensor("pbig%d" % i, [128, 1024], F32)) for i in range(4)]
    banks = [big[i // 2][:, (i % 2) * 512:(i % 2 + 1) * 512] for i in range(8)]
    bankb = [Buf() for _ in range(8)]

    def sem(name):
        return es.enter_context(nc.semaphore(name))

    pe = Eng("pe", sem("s_pe"), is_pe=True)
    act = Eng("act", sem("s_act"))
    dve = Eng("dve", sem("s_dve"))
    pool = Eng("pool", sem("s_pool"))
    sp = Eng("sp", sem("s_sp"))
    sp.pool = [[sem("dq_sp%d" % i), 0] for i in range(16)]
    pool.pool = [[sem("dq_pl%d" % i), 0] for i in range(16)]
    engines = [pe, act, dve, pool, sp]

    st = {"off": 0}

    def alloc(shape, dt):
        n = int(np.prod(shape[1:]))
        nb = n * (4 if dt == F32 else 2)
        nb = (nb + 63) // 64 * 64
        off = st["off"]
        st["off"] += nb // 2
        assert st["off"] <= ARENA, ("arena overflow", st["off"])
        v = arena[0:shape[0], off:off + nb // 2]
        if dt == F32:
            v = v.bitcast(F32)
        v = v[:, 0:n]
        if len(shape) == 3:
            v = v.rearrange("p (a b) -> p a b", a=shape[1])
        elif len(shape) == 4:
            v = v.rearrange("p (a b c) -> p a b c", a=shape[1], b=shape[2])
        return v

    def barrier():
        toks = []
        for e in engines:
            if e.n > 0:
                toks.append((e.sem, e.n))
            if e.pool:
                for (s, v) in e.pool:
                    if v > 0:
                        toks.append((s, v))
        for e in engines:
            waits = []
            for (s, v) in toks:
                if e.is_pe and s is e.sem:
                    continue
                if e.seen.get(id(s), 0) >= v:
                    continue
                e.seen[id(s)] = v
                waits.append((s, v))

            def emit(eng, waits=waits):
                for (s, v) in waits:
                    eng.wait_ge(s, v)
            e.prog.append(emit)

    ident_b = alloc([128, 128], BF16)
    ident_f = alloc([128, 128], F32)
    identB = Buf()
    pool.op(lambda e: e.memset(ident_f, 0.0), writes=[identB])
    pool.op(lambda e: e.affine_select(out=ident_f, in_=ident_f, pattern=[[-1, 128]],
                                      compare_op=ALU.not_equal, fill=1.0, base=0,
                                      channel_multiplier=1), reads=[identB], writes=[identB])
    pool.op(lambda e: e.tensor_copy(out=ident_b, in_=ident_f), reads=[identB], writes=[identB])
    base_off = st["off"]

    def colgain(name):
        t = alloc([128, 8], F32)
        t8 = alloc([8, 128], F32)
        b = Buf(); b8 = Buf()
        sp.dma(t8, dr[name].rearrange("(c p) -> c p", p=128), writes=[b8])
        pe.op(lambda e: e.transpose(banks[0][:, 0:8], t8, ident_f[0:8, 0:8]), reads=[b8, identB], writes=[bankb[0]])
        act.op(lambda e: e.activation(out=t, in_=banks[0][:, 0:8], func=AF.Copy), reads=[bankb[0]], writes=[b])
        return t, b

    def rowgain(name):
        t = alloc([128, D], F32)
        b = Buf()
        sp.dma(t, dr[name].partition_broadcast(128), writes=[b])
        return t, b

    def rstd_ops(ss, out, n, invn, rb, wb):
        act.op(lambda e: e.activation(out=out[:, 0:n], in_=ss[:, 0:n], func=AF.Ln, scale=invn, bias=EPS),
               reads=rb, writes=wb)
        act.op(lambda e: e.activation(out=out[:, 0:n], in_=out[:, 0:n], func=AF.Exp, scale=-0.5),
               reads=wb, writes=wb)

    def load_weight(dst, src, nchunk, b):
        for c in range(nchunk):
            pool.dma(dst[:, c, :], src[c * 128:(c + 1) * 128, :], writes=[b])

    def norm_transpose(r, rB, ns, xn, xnB, xnT, xnTB, gcol, gcolB, ss, rs, sB, junk, junkB, pb):
        for s in range(ns):
            act.op(lambda e, s=s: e.activation(out=junk, in_=r[:, s, :], func=AF.Square,
                                               accum_out=ss[:, s:s + 1]),
                   reads=[rB[s]], writes=[junkB, sB])
        rstd_ops(ss, rs, ns, 1.0 / D, [sB], [sB])
        for s in range(ns):
            dve.op(lambda e, s=s: e.tensor_scalar(out=xn[:, s, :], in0=r[:, s, :], scalar1=rs[:, s:s + 1],
                                                  scalar2=None, op0=ALU.mult),
                   reads=[rB[s], sB], writes=[xnB[s]])
        for dc in range(DC):
            bi = pb[dc % 2]
            pt = banks[bi][:].bitcast(BF16)

            def tr(e, dc=dc, pt=pt):
                ins = None
                for s in range(ns):
                    ins = e.transpose(pt[:, s * 128:(s + 1) * 128], xn[:, s, dc * 128:(dc + 1) * 128], ident_b)
                return ins
            pe.op(tr, reads=[xnB[s] for s in range(ns)] + [identB], writes=[bankb[bi]])
            if dc % 2 == 0:
                act.op(lambda e, dc=dc, pt=pt: e.activation(out=xnT[:, dc, 0:ns * 128], in_=pt[:, 0:ns * 128],
                                                            func=AF.Copy, scale=gcol[:, dc:dc + 1]),
                       reads=[bankb[bi], gcolB], writes=[xnTB])
            else:
                dve.op(lambda e, dc=dc, pt=pt: e.tensor_scalar(out=xnT[:, dc, 0:ns * 128], in0=pt[:, 0:ns * 128],
                                                               scalar1=gcol[:, dc:dc + 1], scalar2=None, op0=ALU.mult),
                       reads=[bankb[bi], gcolB], writes=[xnTB])

    def post_resid(pyb, r, rB, s, ysb, ysbB, junk, junkB, ss, rs, sB, grow, growB, half):
        py = banks[pyb[0]]
        py1 = banks[pyb[1]]
        act.op(lambda e: e.activation(out=ysb[:, 0:512], in_=py[:], func=AF.Copy), reads=[bankb[pyb[0]]], writes=[ysbB])
        dve.op(lambda e: e.tensor_copy(out=ysb[:, 512:1024], in_=py1[:]), reads=[bankb[pyb[1]]], writes=[ysbB])
        act.op(lambda e: e.activation(out=junk, in_=ysb, func=AF.Square, accum_out=ss[:, 0:1]),
               reads=[ysbB], writes=[junkB, sB])
        rstd_ops(ss, rs, 1, 1.0 / D, [sB], [sB])
        dve.op(lambda e: e.scalar_tensor_tensor(out=ysb, in0=ysb, scalar=rs[:, 0:1], in1=grow,
                                                op0=ALU.mult, op1=ALU.mult),
               reads=[ysbB, sB, growB], writes=[ysbB])
        dve.op(lambda e: e.scalar_tensor_tensor(out=r[:, s, :], in0=ysb, scalar=half, in1=r[:, s, :],
                                                op0=ALU.mult, op1=ALU.add),
               reads=[ysbB, rB[s]], writes=[rB[s]])

    def ffn_phase(tiles, pfx, final):
        st["off"] = base_off
        wg = alloc([128, DC, FF], BF16); wu = alloc([128, DC, FF], BF16); wd = alloc([128, FC, D], BF16)
        wgB, wuB, wdB = Buf(), Buf(), Buf()
        load_weight(wg, dr[pfx + "_wg"], DC, wgB)
        load_weight(wu, dr[pfx + "_wu"], DC, wuB)
        load_weight(wd, dr[pfx + "_wd"], FC, wdB)
        gpre, gpreB = colgain(pfx + "_pre")
        gpost, gpostB = rowgain(pfx + "_post")
        if final:
            gfin, gfinB = rowgain("final_norm")
        r = alloc([128, 4, D], F32); rB = [Buf() for _ in range(4)]
        xn = alloc([128, 4, D], BF16); xnB = [Buf() for _ in range(4)]
        xnB[1] = xnB[0]
        ysb = xn[:, 0:2, :].rearrange("p a b -> p (a b)").bitcast(F32)
        ysbB = xnB[0]
        xnT = alloc([128, DC, 512], BF16); xnTB = Buf()
        actT = alloc([128, FC, 512], BF16); actTB = Buf()
        sg = alloc([128, 512], F32); sgB = Buf()
        junk = alloc([128, D], BF16); junkB = Buf()
        ss = alloc([128, 4], F32); rs = alloc([128, 4], F32); sB = Buf()
        ss2 = alloc([128, 4], F32); rs2 = alloc([128, 4], F32); s2B = Buf()
        for ti, (src, dst, ns) in enumerate(tiles):
            N = ns * 128
            if ti == 0:
                sp.dma(r[:, 0:ns, :], src.rearrange("(s p) d -> p s d", p=128), writes=rB[0:ns])
            nxt = tiles[ti + 1] if ti + 1 < len(tiles) else None
            norm_transpose(r, rB, ns, xn, xnB, xnT, xnTB, gpre, gpreB, ss, rs, sB, junk, junkB, (0, 1))
            for fc in range(FC):
                bg = 2 + 2 * (fc % 2)
                bu = bg + 1

                def mm(e, fc=fc, bg=bg, bu=bu, N=N):
                    ins = None
                    for dc in range(DC):
                        ins = e.matmul(banks[bg][:, 0:N], lhsT=wg[:, dc, fc * 128:(fc + 1) * 128], rhs=xnT[:, dc, 0:N],
                                       start=(dc == 0), stop=(dc == DC - 1))
                    for dc in range(DC):
                        ins = e.matmul(banks[bu][:, 0:N], lhsT=wu[:, dc, fc * 128:(fc + 1) * 128], rhs=xnT[:, dc, 0:N],
                                       start=(dc == 0), stop=(dc == DC - 1))
                    return ins
                pe.op(mm, reads=[wgB, wuB, xnTB], writes=[bankb[bg], bankb[bu]])
                act.op(lambda e, bg=bg, N=N: e.activation(out=sg[:, 0:N], in_=banks[bg][:, 0:N], func=AF.Silu),
                       reads=[bankb[bg]], writes=[sgB])
                dve.op(lambda e, fc=fc, bu=bu, N=N: e.tensor_tensor(out=actT[:, fc, 0:N], in0=sg[:, 0:N],
                                                                   in1=banks[bu][:, 0:N], op=ALU.mult),
                       reads=[sgB, bankb[bu]], writes=[actTB])
            for s in range(ns):
                pyb = (6, 7)

                def dn(e, s=s):
                    ins = None
                    for hf in range(2):
                        for fc in range(FC):
                            ins = e.matmul(banks[6 + hf][:], lhsT=actT[:, fc, s * 128:(s + 1) * 128],
                                           rhs=wd[:, fc, hf * 512:(hf + 1) * 512], start=(fc == 0), stop=(fc == FC - 1))
                    return ins
                pe.op(dn, reads=[actTB, wdB], writes=[bankb[6], bankb[7]])
                post_resid(pyb, r, rB, s, ysb, ysbB, junk, junkB, ss2, rs2, s2B, gpost, gpostB, 0.5)
                if final:
                    act.op(lambda e, s=s: e.activation(out=junk, in_=r[:, s, :], func=AF.Square, accum_out=ss2[:, 1:2]),
                           reads=[rB[s]], writes=[junkB, s2B])
                    act.op(lambda e: e.activation(out=rs2[:, 1:2], in_=ss2[:, 1:2], func=AF.Ln, scale=1.0 / D, bias=EPS),
                           reads=[s2B], writes=[s2B])
                    act.op(lambda e: e.activation(out=rs2[:, 1:2], in_=rs2[:, 1:2], func=AF.Exp, scale=-0.5),
                           reads=[s2B], writes=[s2B])
                    dve.op(lambda e, s=s: e.scalar_tensor_tensor(out=r[:, s, :], in0=r[:, s, :], scalar=rs2[:, 1:2],
                                                                 in1=gfin, op0=ALU.mult, op1=ALU.mult),
                           reads=[rB[s], s2B, gfinB], writes=[rB[s]])
                sp.dma(dst[s * 128:(s + 1) * 128, :], r[:, s, :], reads=[rB[s]])
                if nxt is not None and s < nxt[2]:
                    sp.dma(r[:, s, :], nxt[0][s * 128:(s + 1) * 128, :], writes=[rB[s]])
            if nxt is not None:
                for s in range(ns, nxt[2]):
                    sp.dma(r[:, s, :], nxt[0][s * 128:(s + 1) * 128, :], writes=[rB[s]])
        barrier()

    def proj_phase():
        st["off"] = base_off
        win = alloc([128, DC, 2304], BF16); winB = Buf()
        wi = dr["w_in"]
        for c in range(DC):
            rows = wi[c * 128:(c + 1) * 128, :]
            pool.dma(win[:, c, 0:1536], rows[:, 0:1536], writes=[winB])
            srcq = rows[:, 1536:2048].rearrange("p (g j d) -> p j g d", g=2, j=4)
            for j in range(4):
                pool.dma(win[:, c, 1536 + j * 128:1536 + (j + 1) * 128].rearrange("p (g d) -> p g d", g=2),
                         srcq[:, j], writes=[winB])
            pool.dma(win[:, c, 2048:2304], rows[:, 2048:2304], writes=[winB])
        gmix, gmixB = colgain("mix_pre")
        gqk = alloc([128, 10, 64], F32); gqkB = Buf()
        g2 = alloc([128, 2, 64], F32); g2B = Buf()
        sp.dma(g2[:, 0, :], dr["q_norm"].partition_broadcast(128), writes=[g2B])
        sp.dma(g2[:, 1, :], dr["k_norm"].partition_broadcast(128), writes=[g2B])
        dve.op(lambda e: e.tensor_copy(out=gqk[:, 0:8, :], in_=g2[:, 0:1, :].broadcast_to([128, 8, 64])), reads=[g2B], writes=[gqkB])
        dve.op(lambda e: e.tensor_copy(out=gqk[:, 8:10, :], in_=g2[:, 1:2, :].broadcast_to([128, 2, 64])), reads=[g2B], writes=[gqkB])
        act.op(lambda e: e.activation(out=gqk[:, 0:8, :], in_=gqk[:, 0:8, :], func=AF.Copy, scale=0.125),
               reads=[gqkB], writes=[gqkB])
        r = alloc([128, 4, D], F32); rB = [Buf() for _ in range(4)]
        xn = alloc([128, 4, D], BF16); xnB = [Buf() for _ in range(4)]
        xnT = alloc([128, DC, 512], BF16); xnTB = Buf()
        junk = alloc([128, D], BF16); junkB = Buf()
        ss = alloc([128, 4], F32); rs = alloc([128, 4], F32); sB = Buf()
        rope = alloc([128, 4, 128], F32); ropeB = Buf()
        stq = alloc([128, 8, 512], BF16); stqB = Buf()
        vast = alloc([128, 4, 8, 80], BF16); vastB = Buf()
        vbst = alloc([128, 4, 2, 80], BF16); vbstB = Buf()
        qbst = alloc([128, 4, 512], BF16); qbstB = Buf()
        kbst = alloc([128, 512], BF16); kbstB = Buf()
        qk = alloc([128, 4, 10, 64], F32); qkB = Buf()
        t1 = alloc([128, 4, 10, 64], F32); t1B = Buf()
        t2 = alloc([128, 4, 10, 64], F32); t2B = Buf()
        qr = alloc([128, 4, 10, 64], BF16); qrB = Buf()
        ssq = alloc([128, 4, 10], F32); rq = alloc([128, 4, 10], F32); sqB = Buf()
        pool.op(lambda e: e.memset(vast, 1.0), writes=[vastB])
        pool.op(lambda e: e.memset(vbst, 1.0), writes=[vbstB])
        KP2 = int(os.environ.get("KP2", "9"))

        tiles = []
        for t in range(14):
            ns = 4 if t < 13 else 1
            tiles.append(("own", t * 512, ns))
        for t in range(NSF // 512):
            tiles.append(("full", t * 512, 4))
        if os.environ.get("KSMALL"):
            tiles = tiles[:15]
        if KP2 < 2:
            tiles = []
        r2 = [r, alloc([128, 4, D], F32)]; r2B = [rB, [Buf() for _ in range(4)]]
        rope2 = [rope, alloc([128, 4, 128], F32)]; rope2B = [ropeB, Buf()]

        def p2_load(tile, bi):
            kind_, t0_, ns_ = tile
            hs = dr["h1_own"] if kind_ == "own" else dr["h1_full"]
            rsrc_ = dr["rope_own"] if kind_ == "own" else dr["rope_full"]
            sp.dma(r2[bi][:, 0:ns_, :], hs[t0_:t0_ + ns_ * 128, :].rearrange("(s p) d -> p s d", p=128), writes=r2B[bi][0:ns_])
            sp.dma(rope2[bi][:, 0:ns_, :], rsrc_[t0_:t0_ + ns_ * 128, :].rearrange("(s p) d -> p s d", p=128), writes=[rope2B[bi]])
        for tix, (kind, t0, ns) in enumerate(tiles):
            N = ns * 128
            full = kind == "own"
            hsrc = dr["h1_own"] if full else dr["h1_full"]
            rsrc = dr["rope_own"] if full else dr["rope_full"]
            r = r2[tix % 2]; rB = r2B[tix % 2]; rope = rope2[tix % 2]; ropeB = rope2B[tix % 2]
            if tix == 0:
                p2_load(tiles[0], 0)
            if tix + 1 < len(tiles):
                p2_load(tiles[tix + 1], (tix + 1) % 2)
            norm_transpose(r, rB, ns, xn, xnB, xnT, xnTB, gmix, gmixB, ss, rs, sB, junk, junkB, (0, 1))
            if KP2 < 3:
                continue
            if full:
                for p in range(8):
                    bi = 2 + (p % 2)
                    col = (0 if p < 4 else 512) + (p % 4) * 128

                    def mmq(e, bi=bi, col=col, N=N):
                        ins = None
                        for dc in range(DC):
                            ins = e.matmul(banks[bi][:, 0:N], lhsT=win[:, dc, col:col + 128], rhs=xnT[:, dc, 0:N],
                                           start=(dc == 0), stop=(dc == DC - 1))
                        return ins
                    pe.op(mmq, reads=[winB, xnTB], writes=[bankb[bi]])
                    if p % 2 == 0:
                        act.op(lambda e, p=p, bi=bi, N=N: e.activation(out=stq[:, p, 0:N], in_=banks[bi][:, 0:N], func=AF.Copy),
                               reads=[bankb[bi]], writes=[stqB])
                    else:
                        dve.op(lambda e, p=p, bi=bi, N=N: e.tensor_copy(out=stq[:, p, 0:N], in_=banks[bi][:, 0:N]),
                               reads=[bankb[bi]], writes=[stqB])
                if t0 < NQ:
                    sp.dma(dr["qaT"][:, :, t0:t0 + N], stq[:, 0:4, 0:N], reads=[stqB])
                sp.dma(dr["kaT"][:, :, t0:t0 + N], stq[:, 4:8, 0:N], reads=[stqB])
            for s in range(ns):
                bva, bqb, bkv = (4, 5, 6) if s % 2 == 0 else (2, 3, 7)
                if full:
                    def mmv(e, s=s, bva=bva):
                        ins = None
                        for dc in range(DC):
                            ins = e.matmul(banks[bva][:], lhsT=xnT[:, dc, s * 128:(s + 1) * 128], rhs=win[:, dc, 1024:1536],
                                           start=(dc == 0), stop=(dc == DC - 1))
                        return ins
                    pe.op(mmv, reads=[winB, xnTB], writes=[bankb[bva]])
                    act.op(lambda e, s=s, bva=bva: e.activation(out=vast[:, s, :, 0:64],
                                                                in_=banks[bva][:].rearrange("p (h d) -> p h d", h=8), func=AF.Copy),
                           reads=[bankb[bva]], writes=[vastB])

                    def mmqb(e, s=s, bqb=bqb):
                        ins = None
                        for dc in range(DC):
                            ins = e.matmul(banks[bqb][:], lhsT=xnT[:, dc, s * 128:(s + 1) * 128], rhs=win[:, dc, 1536:2048],
                                           start=(dc == 0), stop=(dc == DC - 1))
                        return ins
                    pe.op(mmqb, reads=[winB, xnTB], writes=[bankb[bqb]])
                    act.op(lambda e, s=s, bqb=bqb: e.activation(out=qk[:, s, 0:8, :], in_=banks[bqb][:].rearrange("p (h d) -> p h d", h=8),
                                                                func=AF.Copy), reads=[bankb[bqb]], writes=[qkB])

                def mmkv(e, s=s, bkv=bkv):
                    ins = None
                    for dc in range(DC):
                        ins = e.matmul(banks[bkv][:, 0:256], lhsT=xnT[:, dc, s * 128:(s + 1) * 128], rhs=win[:, dc, 2048:2304],
                                       start=(dc == 0), stop=(dc == DC - 1))
                    return ins
                pe.op(mmkv, reads=[winB, xnTB], writes=[bankb[bkv]])
                dve.op(lambda e, s=s, bkv=bkv: e.tensor_copy(out=qk[:, s, 8:10, :], in_=banks[bkv][:, 0:128].rearrange("p (h d) -> p h d", h=2)),
                       reads=[bankb[bkv]], writes=[qkB])
                dve.op(lambda e, s=s, bkv=bkv: e.tensor_copy(out=vbst[:, s, :, 0:64],
                                                             in_=banks[bkv][:, 128:256].rearrange("p (h d) -> p h d", h=2)),
                       reads=[bankb[bkv]], writes=[vbstB])
            h0 = 0 if full else 8
            nh = 10 - h0
            Q = qk[:, 0:ns, h0:10, :]
            T1 = t1[:, 0:ns, h0:10, :]
            dve.op(lambda e, Q=Q, T1=T1: e.tensor_tensor(out=T1, in0=Q, in1=Q, op=ALU.mult), reads=[qkB], writes=[t1B])
            dve.op(lambda e, T1=T1, h0=h0, ns=ns: e.tensor_reduce(out=ssq[:, 0:ns, h0:10], in_=T1, axis=AX.X, op=ALU.add),
                   reads=[t1B], writes=[sqB])
            act.op(lambda e, h0=h0, ns=ns: e.activation(out=rq[:, 0:ns, h0:10], in_=ssq[:, 0:ns, h0:10], func=AF.Ln, scale=1.0 / 64, bias=EPS),
                   reads=[sqB], writes=[sqB])
            act.op(lambda e, h0=h0, ns=ns: e.activation(out=rq[:, 0:ns, h0:10], in_=rq[:, 0:ns, h0:10], func=AF.Exp, scale=-0.5),
                   reads=[sqB], writes=[sqB])
            for s in range(ns):
                dve.op(lambda e, s=s, h0=h0, nh=nh: e.tensor_tensor(out=t1[:, s, h0:10, :], in0=qk[:, s, h0:10, :],
                                                                   in1=rq[:, s, h0:10].unsqueeze(2).broadcast_to([128, nh, 64]), op=ALU.mult),
                       reads=[qkB, sqB, t1B], writes=[t1B])
            dve.op(lambda e, Q=Q, T1=T1, h0=h0, ns=ns, nh=nh: e.tensor_tensor(
                out=Q, in0=T1, in1=gqk[:, h0:10, :].unsqueeze(1).broadcast_to([128, ns, nh, 64]), op=ALU.mult),
                reads=[t1B, gqkB], writes=[qkB])
            Cb = rope[:, 0:ns, 0:64].unsqueeze(2).broadcast_to([128, ns, nh, 64])
            dve.op(lambda e, Q=Q, T1=T1, Cb=Cb: e.tensor_tensor(out=T1, in0=Q, in1=Cb, op=ALU.mult),
                   reads=[qkB, ropeB], writes=[t1B])
            for hf in range(2):
                for xi in range(2):
                    lo = hf * 32 + xi * 16
                    so = hf * 32 + (1 - xi) * 16
                    Sb = rope[:, 0:ns, 64 + lo:64 + lo + 16].unsqueeze(2).broadcast_to([128, ns, nh, 16])
                    dve.op(lambda e, h0=h0, lo=lo, so=so, Sb=Sb, ns=ns: e.tensor_tensor(
                        out=t2[:, 0:ns, h0:10, lo:lo + 16], in0=qk[:, 0:ns, h0:10, so:so + 16], in1=Sb, op=ALU.mult),
                        reads=[qkB, ropeB], writes=[t2B])
            dve.op(lambda e, h0=h0, ns=ns, T1=T1: e.tensor_tensor(out=qr[:, 0:ns, h0:10, :], in0=T1, in1=t2[:, 0:ns, h0:10, :], op=ALU.add),
                   reads=[t1B, t2B], writes=[qrB])
            for s in range(ns):
                qrf = qr[:, s].rearrange("p h d -> p (h d)")
                pbi = 7 if s % 2 == 0 else 4
                ptb = banks[pbi][:].bitcast(BF16)
                if full:
                    def trq(e, qrf=qrf, ptb=ptb):
                        ins = None
                        for j in range(4):
                            ins = e.transpose(ptb[:, j * 128:(j + 1) * 128], qrf[:, j * 128:(j + 1) * 128], ident_b)
                        ins = e.transpose(ptb[:, 512:640], qrf[:, 512:640], ident_b)
                        return ins
                    pe.op(trq, reads=[qrB, identB], writes=[bankb[pbi]])
                    act.op(lambda e, s=s, ptb=ptb: e.activation(out=qbst[:, :, s * 128:(s + 1) * 128],
                                                                in_=ptb[:, 0:512].rearrange("p (j t) -> p j t", j=4), func=AF.Copy),
                           reads=[bankb[pbi]], writes=[qbstB])
                else:
                    pe.op(lambda e, qrf=qrf, ptb=ptb: e.transpose(ptb[:, 512:640], qrf[:, 512:640], ident_b),
                          reads=[qrB, identB], writes=[bankb[pbi]])
                act.op(lambda e, s=s, ptb=ptb: e.activation(out=kbst[:, s * 128:(s + 1) * 128], in_=ptb[:, 512:640], func=AF.Copy),
                       reads=[bankb[pbi]], writes=[kbstB])
            if KP2 < 7:
                continue
            KST = int(os.environ.get("KST", "63"))
            if full:
                if KST & 1:
                    sp.dma(dr["va"][t0:t0 + N, :].rearrange("(s p) c -> p s c", p=128),
                           vast[:, 0:ns].rearrange("p s h d -> p s (h d)"), reads=[vastB])
                if t0 < NQ and (KST & 2):
                    sp.dma(dr["qbT"][:, :, t0:t0 + N], qbst[:, :, 0:N], reads=[qbstB])
                if KST & 4:
                    KBV = int(os.environ.get("KBV", "0"))
                    if KBV == 1:
                        pool.dma(dr["kbT_own"][:, t0:t0 + N], kbst[:, 0:N], reads=[kbstB])
                    elif KBV == 2:
                        if N == 512:
                            sp.dma(dr["kbT_own"][:, t0:t0 + N], kbst[:, 0:N], reads=[kbstB])
                    elif KBV == 4:
                        sp.dma(dr["kbT_own"][:, t0:t0 + N], kbst[:, 0:N], reads=[])
                    elif KBV == 5:
                        if t0 != 6656:
                            sp.dma(dr["kbT_own"][:, t0:t0 + N], kbst[:, 0:N], reads=[kbstB])
                    elif KBV == 3:
                        sp.dma(dr["kbT_own"][:, t0:t0 + N].unsqueeze(1), kbst[:, 0:N].unsqueeze(1), reads=[kbstB])
                    else:
                        sp.dma(dr["kbT_own"][:, t0:t0 + N], kbst[:, 0:N], reads=[kbstB])
                if KST & 8:
                    sp.dma(dr["vb_own"][t0:t0 + N, :].rearrange("(s p) c -> p s c", p=128),
                           vbst[:, 0:ns].rearrange("p s h d -> p s (h d)"), reads=[vbstB])
            else:
                if KST & 16:
                    sp.dma(dr["kbT_full"][:, t0:t0 + N], kbst[:, 0:N], reads=[kbstB])
                if KST & 32:
                    sp.dma(dr["vb_full"][t0:t0 + N, :].rearrange("(s p) c -> p s c", p=128),
                           vbst[:, 0:ns].rearrange("p s h d -> p s (h d)"), reads=[vbstB])
        barrier()

    def attn_phase():
        st["off"] = base_off
        KTM = 129
        kT = alloc([128, 128 * 128 + 16], BF16); kTB = Buf()
        vS = alloc([128, KTM, 160], BF16); vSB = Buf()
        kaM = alloc([128, 4, 16], BF16); vaM = alloc([16, 640], BF16); mbT = alloc([16, 8], F32); metaB = Buf()
        sp.dma(kaM, dr["kaT"][:, :, NQ:NQ + 16], writes=[metaB])
        sp.dma(vaM, dr["va"][NQ:NQ + 16, :], writes=[metaB])
        mb8 = alloc([8, 16], F32); mb8B = Buf()
        sp.dma(mb8, dr["meta_bias"], writes=[mb8B])
        pe.op(lambda e: e.transpose(banks[0][0:16, 0:8], mb8, ident_f[0:8, 0:8]), reads=[mb8B, identB], writes=[bankb[0]])
        act.op(lambda e: e.activation(out=mbT, in_=banks[0][0:16, 0:8], func=AF.Copy), reads=[bankb[0]], writes=[metaB])
        qa2 = [alloc([128, 4, 256], BF16) for _ in range(2)]; qa2B = [Buf(), Buf()]
        kaw2 = [alloc([128, 4, 768], BF16) for _ in range(2)]; kaw2B = [Buf(), Buf()]
        vaw2 = [alloc([128, 6, 640], BF16) for _ in range(2)]; vaw2B = [Buf(), Buf()]
        qb2 = [alloc([128, 4, 256], BF16) for _ in range(2)]; qb2B = [Buf(), Buf()]
        tab = [alloc([128, 6, 2, 256], F32) for _ in range(2)]; tabB = [Buf(), Buf()]
        tmp = [alloc([128, 512], F32) for _ in range(4)]; tmpB = [Buf() for _ in range(4)]
        pn = [alloc([128, 512], BF16) for _ in range(4)]; pnB = [Buf() for _ in range(4)]
        pg2 = [alloc([128, 1024], BF16) for _ in range(3)]; pg2B = [Buf() for _ in range(3)]
        oa2 = [alloc([65, 8, 256], F32) for _ in range(2)]; oa2B = [Buf(), Buf()]
        ob = alloc([65, 2, 2, 512], F32); obB = Buf()
        yc = alloc([128, 16, 64], F32); ycB = Buf()
        rc = alloc([128, 16], F32); rcB = Buf()
        yo = alloc([128, D], BF16); yoB = Buf()
        junk = alloc([128, 512], BF16); junkB = Buf()
        ss = alloc([128, 2], F32); rs = alloc([128, 2], F32); sB = Buf()
        cnt = {"s": 0, "g": 0, "blk": 0}

        def run_seq(segs, blocks):
            nkt = sum(n for (_k, _v, n) in segs) // 128
            NK = nkt * 128
            base = 0
            for (kt_src, v_src, n) in segs:
                for c in range(0, n, 2048):
                    sp.dma(kT[:, base + c:base + c + 2048], kt_src[:, c:c + 2048], writes=[kTB])
                for c in range(0, n // 128, 16):
                    sp.dma(vS[:, base // 128 + c:base // 128 + c + 16, :],
                           v_src[c * 128:(c + 16) * 128, :].rearrange("(t p) c -> p t c", p=128), writes=[vSB])
                base += n
            sp.dma(kT[:, NK:NK + 16], dr["kbT_own"][:, NQ:NQ + 16], writes=[kTB])
            sp.dma(vS[0:16, nkt, :], dr["vb_own"][NQ:NQ + 16, :], writes=[vSB])
            def blk_load(blk, bi):
                q0_, w0_, _t = blk
                sp.dma(qa2[bi], dr["qaT"][:, :, q0_:q0_ + 256], writes=[qa2B[bi]])
                sp.dma(kaw2[bi], dr["kaT"][:, :, w0_:w0_ + 768], writes=[kaw2B[bi]])
                sp.dma(vaw2[bi], dr["va"][w0_:w0_ + 768, :].rearrange("(t p) c -> p t c", p=128), writes=[vaw2B[bi]])
                sp.dma(qb2[bi], dr["qbT"][:, :, q0_:q0_ + 256], writes=[qb2B[bi]])
            def do_block(q0, w0, typ, qa, qaB, kaw, kawB, vaw, vawB, qb, qbB, oa, oaB):
                steps = [(p, o) for p in range(4) for o in range(7)]

                def na_front(i):
                    p, o = steps[i]
                    tb = tab[p % 2]; tbB = tabB[p % 2]
                    if o == 0:
                        for e2 in range(2):
                            sp.dma(tb[:, :, e2, :], dr["na_tab"][typ, 2 * p + e2].rearrange("k (o q) -> k o q", o=6), writes=[tbB])
                    j = cnt["g"] % 2; cnt["g"] += 1
                    k = i % 4
                    sb2 = [bankb[2 * j], bankb[2 * j + 1]]
                    if o < 6:
                        def qk(e):
                            ins = None
                            for e2 in range(2):
                                ins = e.matmul(banks[2 * j + e2][:, 0:256],
                                               lhsT=kaw[64 * e2:64 * e2 + 64, p, o * 128:(o + 1) * 128],
                                               rhs=qa[64 * e2:64 * e2 + 64, p, :], start=True, stop=True)
                            return ins
                        pe.op(qk, reads=[kawB, qaB], writes=sb2)
                        dve.op(lambda e: e.scalar_tensor_tensor(
                            out=tmp[k].rearrange("k (e q) -> k e q", e=2),
                            in0=big[j][:, :].rearrange("k (e q) -> k e q", e=2)[:, :, 0:256], scalar=0.125, in1=tb[:, o],
                            op0=ALU.mult, op1=ALU.add), reads=sb2 + [tbB], writes=[tmpB[k]])
                        act.op(lambda e: e.activation(out=pn[k], in_=tmp[k], func=AF.Exp),
                               reads=[tmpB[k]], writes=[pnB[k]])
                    else:
                        def qkm(e):
                            ins = None
                            for e2 in range(2):
                                ins = e.matmul(banks[2 * j + e2][0:16, 0:256], lhsT=kaM[64 * e2:64 * e2 + 64, p, :],
                                               rhs=qa[64 * e2:64 * e2 + 64, p, :], start=True, stop=True)
                            return ins
                        pe.op(qkm, reads=[metaB, qaB], writes=sb2)
                        for e2 in range(2):
                            h = 2 * p + e2
                            act.op(lambda e, e2=e2, h=h: e.activation(
                                out=pn[k][0:16, e2 * 256:(e2 + 1) * 256], in_=banks[2 * j + e2][0:16, 0:256],
                                func=AF.Exp, scale=0.125, bias=mbT[:, h:h + 1]), reads=[bankb[2 * j + e2], metaB], writes=[pnB[k]])

                def na_back(i):
                    p, o = steps[i]
                    k = i % 4

                    def pv(e):
                        ins = None
                        for e2 in range(2):
                            h = 2 * p + e2
                            if o < 6:
                                ins = e.matmul(banks[4 + e2][0:65, 0:256], lhsT=vaw[:, o, h * 80:h * 80 + 65],
                                               rhs=pn[k][:, e2 * 256:(e2 + 1) * 256], start=(o == 0), stop=False)
                            else:
                                ins = e.matmul(banks[4 + e2][0:65, 0:256], lhsT=vaM[:, h * 80:h * 80 + 65],
                                               rhs=pn[k][0:16, e2 * 256:(e2 + 1) * 256], start=False, stop=True)
                        return ins
                    pe.op(pv, reads=[vawB, metaB, pnB[k]], writes=[bankb[4], bankb[5]])
                    if o == 6:
                        act.op(lambda e: e.activation(out=oa[:, 2 * p, :], in_=banks[4][0:65, 0:256], func=AF.Copy),
                               reads=[bankb[4]], writes=[oaB])
                        dve.op(lambda e: e.tensor_copy(out=oa[:, 2 * p + 1, :], in_=banks[5][0:65, 0:256]),
                               reads=[bankb[5]], writes=[oaB])
                LN = 3
                for t in range(len(steps) + LN):
                    if t < len(steps):
                        na_front(t)
                    if t >= LN:
                        na_back(t - LN)
                if cnt.get("pending"):
                    cnt["pending"]()
                    cnt["pending"] = None
                units = [(qt, kt) for qt in range(2) for kt in range(nkt + 1)]

                def g_front(i):
                    qt, kt = units[i]
                    j = cnt["g"] % 2; cnt["g"] += 1
                    k = i % 3
                    kk = 128 if kt < nkt else 16

                    def qk(e):
                        ins = None
                        for g in range(2):
                            ins = e.matmul(banks[2 * j + g][0:kk, :], lhsT=kT[64 * g:64 * g + 64, kt * 128:kt * 128 + kk],
                                           rhs=qb[64 * g:64 * g + 64, :, qt * 128:(qt + 1) * 128], start=True, stop=True)
                        return ins
                    pe.op(qk, reads=[kTB, qbB], writes=[bankb[2 * j], bankb[2 * j + 1]])
                    act.op(lambda e: e.activation(out=pg2[k][0:kk, :], in_=big[j][0:kk, :], func=AF.Exp),
                           reads=[bankb[2 * j], bankb[2 * j + 1]], writes=[pg2B[k]])

                def g_back(i):
                    qt, kt = units[i]
                    k = i % 3
                    kk = 128 if kt < nkt else 16

                    def pv(e):
                        ins = None
                        for g in range(2):
                            ins = e.matmul(banks[4 + g][0:65, :], lhsT=vS[0:kk, kt, g * 80:g * 80 + 65],
                                           rhs=pg2[k][0:kk, g * 512:(g + 1) * 512],
                                           start=(kt == 0), stop=(kt == nkt))
                        return ins
                    pe.op(pv, reads=[vSB, pg2B[k]], writes=[bankb[4], bankb[5]])
                    if kt == nkt:
                        dve.op(lambda e: e.tensor_copy(out=ob[:, 0, qt, :], in_=banks[4][0:65, :]),
                               reads=[bankb[4]], writes=[obB])
                        act.op(lambda e: e.activation(out=ob[:, 1, qt, :], in_=banks[5][0:65, :], func=AF.Copy),
                               reads=[bankb[5]], writes=[obB])
                LG = 1
                for t in range(len(units) + LG):
                    if t < len(units):
                        g_front(t)
                    if t >= LG:
                        g_back(t - LG)
                def fin():
                    for qt in range(2):
                        for grp in range(4):
                            bi = 6 + (grp % 2)
                            pt = banks[bi][:, 0:260].rearrange("p (h c) -> p h c", h=4)

                            def trf(e, grp=grp, qt=qt, pt=pt):
                                ins = None
                                for j in range(4):
                                    if grp < 2:
                                        src = oa[:, grp * 4 + j, qt * 128:(qt + 1) * 128]
                                    else:
                                        src = ob[:, grp - 2, qt, j * 128:(j + 1) * 128]
                                    ins = e.transpose(pt[:, j, :], src, ident_f[0:65, 0:65])
                                return ins
                            pe.op(trf, reads=[oaB, obB, identB], writes=[bankb[bi]])
                            dve.op(lambda e, grp=grp, pt=pt: e.reciprocal(out=rc[:, grp * 4:(grp + 1) * 4], in_=pt[:, :, 64]),
                                   reads=[bankb[bi]], writes=[rcB])
                            dve.op(lambda e, grp=grp, pt=pt: e.tensor_tensor(
                                out=yc[:, grp * 4:(grp + 1) * 4, :], in0=pt[:, :, 0:64],
                                in1=rc[:, grp * 4:(grp + 1) * 4].unsqueeze(2).broadcast_to([128, 4, 64]), op=ALU.mult),
                                reads=[bankb[bi], rcB], writes=[ycB])
                        ycf = yc.rearrange("p h d -> p (h d)")
                        for a in range(2):
                            act.op(lambda e, a=a, ycf=ycf: e.activation(out=junk, in_=ycf[:, a * 512:(a + 1) * 512], func=AF.Square,
                                                                        accum_out=ss[:, a:a + 1]), reads=[ycB], writes=[junkB, sB])
                        rstd_ops(ss, rs, 2, 1.0 / 512, [sB], [sB])
                        for a in range(2):
                            dve.op(lambda e, a=a, ycf=ycf: e.tensor_scalar(out=yo[:, a * 512:(a + 1) * 512], in0=ycf[:, a * 512:(a + 1) * 512],
                                                                           scalar1=rs[:, a:a + 1], scalar2=None, op0=ALU.mult),
                                   reads=[ycB, sB], writes=[yoB])
                        sp.dma(dr["ycat"][q0 + qt * 128:q0 + (qt + 1) * 128, :], yo, reads=[yoB])
                cnt["pending"] = fin


            for bix, blk in enumerate(blocks):
                if bix == 0:
                    blk_load(blocks[0], cnt["blk"] % 2)
                bi = cnt["blk"] % 2; cnt["blk"] += 1
                if bix + 1 < len(blocks):
                    blk_load(blocks[bix + 1], cnt["blk"] % 2)
                do_block(blk[0], blk[1], blk[2], qa2[bi], qa2B[bi], kaw2[bi], kaw2B[bi],
                         vaw2[bi], vaw2B[bi], qb2[bi], qb2B[bi], oa2[bi], oa2B[bi])

        blocks = []
        for b in range(16):
            R = 4 * b
            Rw = min(max(R - 4, 0), 52)
            typ = 1 if b == 0 else (2 if b == 15 else 0)
            blocks.append((R * 64, Rw * 64, typ))
        if os.environ.get("KSMALL"):
            blocks = blocks[:2]
        seg_p = [(dr["kbT_own"][:, 0:NPR], dr["vb_own"][0:NPR, :], NPR)]
        run_seq(seg_p, blocks)
        blocks = []
        for b in range(8):
            Rl = 4 + 4 * b
            typ = 3 if b == 0 else (4 if b == 7 else 0)
            blocks.append((NPR + Rl * 64, NPR + (Rl - 4) * 64, typ))
        o0 = NPR + 256
        seg_s = [(dr["kbT_full"], dr["vb_full"], NSF),
                 (dr["kbT_own"][:, o0:o0 + 2048], dr["vb_own"][o0:o0 + 2048, :], 2048)]
        if os.environ.get("KSMALL"):
            run_seq(seg_p, blocks[:1])
        else:
            run_seq(seg_s, blocks)
        if cnt.get("pending"):
            cnt["pending"]()
            cnt["pending"] = None
        barrier()

    def wout_phase(qtiles):
        st["off"] = base_off
        wo = alloc([128, DC, D], BF16); woB = Buf()
        load_weight(wo, dr["w_out"], DC, woB)
        ggrp, ggrpB = colgain("grp_ab")
        gpost, gpostB = rowgain("mix_post")
        r = alloc([128, 4, D], F32); rB = [Buf() for _ in range(4)]
        yt = alloc([128, 4, D], BF16); ytB = [Buf() for _ in range(4)]
        ycT = alloc([128, DC, 512], BF16); ycTB = Buf()
        ysb = alloc([128, D], F32); ysbB = Buf()
        junk = alloc([128, D], BF16); junkB = Buf()
        ss = alloc([128, 4], F32); rs = alloc([128, 4], F32); sB = Buf()
        rr = [r, alloc([128, 4, D], F32)]; rrB = [rB, [Buf() for _ in range(4)]]
        yy = [yt, alloc([128, 4, D], BF16)]; yyB = [ytB, [Buf() for _ in range(4)]]

        def wo_load(t0_, bi):
            sp.dma(rr[bi], dr["h1_own"][t0_:t0_ + 512, :].rearrange("(s p) d -> p s d", p=128), writes=rrB[bi])
            sp.dma(yy[bi], dr["ycat"][t0_:t0_ + 512, :].rearrange("(s p) d -> p s d", p=128), writes=yyB[bi])
        for tix, t0 in enumerate(qtiles):
            if tix == 0:
                wo_load(qtiles[0], 0)
            if tix + 1 < len(qtiles):
                wo_load(qtiles[tix + 1], (tix + 1) % 2)
            r, rB, yt, ytB = rr[tix % 2], rrB[tix % 2], yy[tix % 2], yyB[tix % 2]
            for dc in range(DC):
                bi = dc % 2
                pt = banks[bi][:].bitcast(BF16)

                def tr(e, dc=dc, pt=pt, yt=yt):
                    ins = None
                    for s in range(4):
                        ins = e.transpose(pt[:, s * 128:(s + 1) * 128], yt[:, s, dc * 128:(dc + 1) * 128], ident_b)
                    return ins
                pe.op(tr, reads=ytB + [identB], writes=[bankb[bi]])
                if dc % 2 == 0:
                    act.op(lambda e, dc=dc, pt=pt: e.activation(out=ycT[:, dc, :], in_=pt[:, 0:512], func=AF.Copy,
                                                                scale=ggrp[:, dc:dc + 1]), reads=[bankb[bi], ggrpB], writes=[ycTB])
                else:
                    dve.op(lambda e, dc=dc, pt=pt: e.tensor_scalar(out=ycT[:, dc, :], in0=pt[:, 0:512], scalar1=ggrp[:, dc:dc + 1],
                                                                   scalar2=None, op0=ALU.mult), reads=[bankb[bi], ggrpB], writes=[ycTB])
            for s in range(4):
                def mo(e, s=s):
                    ins = None
                    for hf in range(2):
                        for dc in range(DC):
                            ins = e.matmul(banks[6 + hf][:], lhsT=ycT[:, dc, s * 128:(s + 1) * 128],
                                           rhs=wo[:, dc, hf * 512:(hf + 1) * 512], start=(dc == 0), stop=(dc == DC - 1))
                    return ins
                pe.op(mo, reads=[ycTB, woB], writes=[bankb[6], bankb[7]])
                post_resid((6, 7), r, rB, s, ysb, ysbB, junk, junkB, ss, rs, sB, gpost, gpostB, 1.0)
            sp.dma(dr["h2"][t0:t0 + 512, :].rearrange("(s p) d -> p s d", p=128), r, reads=rB)
        barrier()

    t1 = []
    for t in range(14):
        ns = 4 if t < 13 else 1
        t1.append((dr["x_own"][t * 512:t * 512 + ns * 128, :], dr["h1_own"][t * 512:t * 512 + ns * 128, :], ns))
    for t in range(NSF // 512):
        t1.append((dr["xs_full"][t * 512:(t + 1) * 512, :], dr["h1_full"][t * 512:(t + 1) * 512, :], 4))
    STOP = int(os.environ.get("KSTOP", "9"))
    if os.environ.get("KSMALL"):
        t1 = t1[:15]
    ffn_phase(t1, "ffn1", False)
    if STOP >= 2:
        proj_phase()
    if STOP >= 3:
        attn_phase()
    qtiles = [t * 512 for t in range(8)] + [NPR + 256 + t * 512 for t in range(4)]
    if os.environ.get("KSMALL"):
        qtiles = qtiles[:1]
    if STOP >= 4:
        wout_phase(qtiles)
    t4 = []
    for t in range(8):
        t4.append((dr["h2"][t * 512:(t + 1) * 512, :], y_p[t * 512:(t + 1) * 512, :], 4))
    for t in range(4):
        a = NPR + 256 + t * 512
        t4.append((dr["h2"][a:a + 512, :], y_s[t * 512:(t + 1) * 512, :], 4))
    if os.environ.get("KSMALL"):
        t4 = t4[:1]
    if STOP >= 5:
        ffn_phase(t4, "ffn2", True)

    with nc.allow_non_contiguous_dma(reason="small strided constant loads"):
        with nc.Block() as block:
            @block.tensor
            def _(e):
                for f in pe.prog:
                    f(e)

            @block.scalar
            def _(e):
                for f in act.prog:
                    f(e)

            @block.vector
            def _(e):
                for f in dve.prog:
                    f(e)

            @block.gpsimd
            def _(e):
                for f in pool.prog:
                    f(e)

            @block.sync
            def _(e):
                for f in sp.prog:
                    f(e)
    es.close()
    return nc


_CACHE = {}


def kernel(**inputs):
    f32 = lambda a: np.ascontiguousarray(np.asarray(a, dtype=np.float32))
    xp = f32(inputs["x_prompt"]); xs = f32(inputs["x_sample"])[0]
    meta = f32(inputs["meta_tokens"])
    rel_bias = f32(inputs["na_rel_bias"])[0]
    shared = {
        "w_in": f32(inputs["w_in"])[0], "w_out": f32(inputs["w_out"])[0],
        "mix_pre": f32(inputs["mix_norm_pre"])[0], "mix_post": f32(inputs["mix_norm_post"])[0],
        "final_norm": f32(inputs["final_norm"]),
        "q_norm": f32(inputs["gqa_q_norm"])[0], "k_norm": f32(inputs["gqa_k_norm"])[0],
        "grp_ab": np.concatenate([f32(inputs["grp_norm_a"])[0], f32(inputs["grp_norm_b"])[0]]),
        "meta_bias": f32(inputs["na_meta_bias"])[0],
    }
    for f in ("ffn1", "ffn2"):
        shared[f + "_wg"] = f32(inputs[f + "_w_gate"])[0]
        shared[f + "_wu"] = f32(inputs[f + "_w_up"])[0]
        shared[f + "_wd"] = f32(inputs[f + "_w_down"])[0]
        shared[f + "_pre"] = f32(inputs[f + "_norm_pre"])[0]
        shared[f + "_post"] = f32(inputs[f + "_norm_post"])[0]
    tall = np.arange(16384)
    rope_all = _rope_table(tall // 64, tall % 64)
    in_maps = []
    for c in range(8):
        x_own = np.zeros((NOWN, D), np.float32)
        x_own[0:NPR] = xp[c]
        g0 = 32 * c - 4
        lo, hi = max(g0, 0), min(g0 + 40, 256)
        x_own[NPR + (lo - g0) * 64:NPR + (hi - g0) * 64] = xs[lo * 64:hi * 64]
        x_own[NQ:NQ + 16] = meta
        prow = np.zeros(NOWN, np.float32); pcol = np.zeros(NOWN, np.float32)
        tp = np.arange(NPR)
        prow[0:NPR] = tp // 64; pcol[0:NPR] = tp % 64
        ts = np.arange(NSC)
        prow[NPR:NQ] = g0 + ts // 64; pcol[NPR:NQ] = ts % 64
        prow[NQ:NQ + 16] = -1.0; pcol[NQ:NQ + 16] = np.arange(16)
        m = dict(shared)
        keep = np.ones(16384, bool)
        keep[2048 * c:2048 * (c + 1)] = False
        m["xs_full"] = np.ascontiguousarray(xs[keep])
        m["rope_full"] = np.ascontiguousarray(rope_all[keep])
        m["x_own"] = x_own
        m["rope_own"] = _rope_table(prow, pcol)
        m["na_tab"] = _na_tables(rel_bias, c)
        in_maps.append(m)
    if "nc" not in _CACHE:
        _CACHE["nc"] = build_program()
    res = run_bass_kernel_spmd(_CACHE["nc"], in_maps, core_ids=list(range(8)))
    y_prompt = np.stack([np.asarray(res.results[c]["y_p"], np.float32) for c in range(8)], 0)
    y_sample = np.concatenate([np.asarray(res.results[c]["y_s"], np.float32) for c in range(8)], 0)[None]
    return (y_prompt, y_sample)
```

```python
import os
import numpy as np
import ml_dtypes
from contextlib import ExitStack
import concourse.bass as bass
import concourse.mybir as mybir
from concourse.bass_utils import run_bass_kernel_spmd

F32 = mybir.dt.float32
BF16 = mybir.dt.bfloat16
AF = mybir.ActivationFunctionType
ALU = mybir.AluOpType
AX = mybir.AxisListType

D = 1024
DC = 8
FF = 2816
FC = 22
NPR = 4096
NSC = 2560
NQ = NPR + NSC
NOWN = NQ + 128
NSF = 16384 - 2048
EPS = 1e-6
NEG = -30000.0
ARENA = 103000


class Buf:
    __slots__ = ("w", "rs")

    def __init__(self):
        self.w = None
        self.rs = []


class Eng:
    def __init__(self, name, sem, is_pe=False):
        self.name = name
        self.sem = sem
        self.n = 0
        self.seen = {}
        self.prog = []
        self.is_pe = is_pe
        self.pool = None
        self.pool_i = 0

    def _deps(self, reads, writes):
        toks = []
        for b in reads:
            if b.w is not None:
                toks.append(b.w)
        for b in writes:
            if b.w is not None:
                toks.append(b.w)
            toks.extend(b.rs)
        waits = []
        for (s, v) in toks:
            if self.is_pe and s is self.sem:
                continue
            if self.seen.get(id(s), 0) >= v:
                continue
            self.seen[id(s)] = v
            waits.append((s, v))
        return waits

    @staticmethod
    def _mark(tok, reads, writes):
        for b in reads:
            b.rs.append(tok)
        for b in writes:
            b.w = tok
            b.rs = []

    def op(self, fn, reads=(), writes=()):
        waits = self._deps(reads, writes)
        self.n += 1
        sem = self.sem

        def emit(e, waits=waits, fn=fn, sem=sem):
            for (s, v) in waits:
                e.wait_ge(s, v)
            fn(e).then_inc(sem, 1)
        self.prog.append(emit)
        tok = (sem, self.n)
        self._mark(tok, reads, writes)
        return tok

    def dma(self, out, in_, reads=(), writes=(), slow=False):
        waits = self._deps(reads, writes)
        ent = self.pool[self.pool_i % len(self.pool)]
        self.pool_i += 1
        sem, prev = ent
        if prev > 0 and self.seen.get(id(sem), 0) < prev:
            self.seen[id(sem)] = prev
            waits.append((sem, prev))
        ent[1] = prev + 16

        def emit(e, waits=waits, out=out, in_=in_, sem=sem, slow=slow):
            for (s, v) in waits:
                e.wait_ge(s, v)
            if slow:
                e.dma_start(out=out, in_=in_, allow_slow_non_contiguous=True).then_inc(sem, 16)
            else:
                e.dma_start(out=out, in_=in_).then_inc(sem, 16)
        self.prog.append(emit)
        tok = (sem, prev + 16)
        self._mark(tok, reads, writes)
        return tok


class K:
    pass


def _na_tables(rel_bias, core):
    rb = np.asarray(rel_bias, np.float32)

    def build(qrows, krow0, rows, kvalid_fn):
        tab = np.full((8, 128, 6, 256), NEG, np.float32)
        kc = np.arange(64)
        qc = np.arange(64)
        c0 = np.clip(qc - 8, 0, 48)
        colok = (kc[:, None] >= c0[None, :]) & (kc[:, None] < c0[None, :] + 16)
        dc = np.clip(kc[:, None] - qc[None, :] + 15, 0, 30)
        for o in range(6):
            for krl in range(2):
                kr = krow0 + 2 * o + krl
                if not kvalid_fn(kr):
                    continue
                for qi, qr in enumerate(qrows):
                    r0 = min(max(qr - 4, 0), rows - 8)
                    if not (r0 <= kr < r0 + 8):
                        continue
                    dr = kr - qr + 7
                    vals = rb[:, dr, :][:, dc]
                    vals = np.where(colok[None], vals, np.float32(NEG))
                    tab[:, krl * 64:(krl + 1) * 64, o, qi * 64:(qi + 1) * 64] = vals
        return tab

    t_int = build([8, 9, 10, 11], 4, 64, lambda r: True)
    t_pf = build([0, 1, 2, 3], 0, 64, lambda r: True)
    t_pl = build([60, 61, 62, 63], 52, 64, lambda r: True)
    g0 = 32 * core - 4
    okr = lambda r: 0 <= r < 256
    t_sf = build([g0 + 4 + i for i in range(4)], g0, 256, okr)
    t_sl = build([g0 + 32 + i for i in range(4)], g0 + 28, 256, okr)
    return np.stack([t_int, t_pf, t_pl, t_sf, t_sl]).reshape(5, 8, 128, 1536)


def _rope_table(pos_row, pos_col):
    inv = (10000.0 ** (-np.arange(0, 32, 2) / 32)).astype(np.float32)
    ar = pos_row.astype(np.float32)[:, None] * inv[None, :]
    ac = pos_col.astype(np.float32)[:, None] * inv[None, :]
    cr, sr, cc, sc = np.cos(ar), np.sin(ar), np.cos(ac), np.sin(ac)
    C = np.concatenate([cr, cr, cc, cc], 1)
    S = np.concatenate([-sr, sr, -sc, sc], 1)
    return np.concatenate([C, S], 1).astype(np.float32)


def build_program():
    nc = bass.Bass("TRN2", target_bir_lowering=False)
    k = K()
    dr = {}

    def din(name, shape, dt=F32):
        dr[name] = nc.dram_tensor(name, list(shape), dt, kind="ExternalInput").ap()

    def dscr(name, shape, dt):
        dr[name] = nc.dram_tensor(name, list(shape), dt).ap()

    din("x_own", [NOWN, D]); din("xs_full", [NSF, D])
    din("rope_own", [NOWN, 128]); din("rope_full", [NSF, 128])
    din("na_tab", [5, 8, 128, 1536])
    for f in ("ffn1", "ffn2"):
        din(f + "_wg", [D, FF]); din(f + "_wu", [D, FF]); din(f + "_wd", [FF, D])
        din(f + "_pre", [D]); din(f + "_post", [D])
    din("w_in", [D, 2304]); din("w_out", [D, D])
    din("mix_pre", [D]); din("mix_post", [D]); din("final_norm", [D])
    din("q_norm", [64]); din("k_norm", [64]); din("grp_ab", [D]); din("meta_bias", [8, 16])
    y_p = nc.dram_tensor("y_p", [NPR, D], F32, kind="ExternalOutput").ap()
    y_s = nc.dram_tensor("y_s", [2048, D], F32, kind="ExternalOutput").ap()
    dscr("h1_own", [NOWN, D], F32); dscr("h1_full", [NSF, D], F32)
    dscr("qaT", [128, 4, NQ], BF16); dscr("kaT", [128, 4, NOWN], BF16)
    dscr("va", [NOWN, 640], BF16)
    dscr("qbT", [128, 4, NQ], BF16); dscr("kbT_own", [128, NOWN], BF16)
    dscr("vb_own", [NOWN, 160], BF16)
    dscr("kbT_full", [128, NSF], BF16); dscr("vb_full", [NSF, 160], BF16)
    dscr("ycat", [NQ, D], BF16); dscr("h2", [NQ, D], F32)

    es = ExitStack()
    arena = es.enter_context(nc.sbuf_tensor("arena", [128, ARENA], BF16))
    big = [es.enter_context(nc.psum_tensor("pbig%d" % i, [128, 1024], F32)) for i in range(4)]
    banks = [big[i // 2][:, (i % 2) * 512:(i % 2 + 1) * 512] for i in range(8)]
    bankb = [Buf() for _ in range(8)]

    def sem(name):
        return es.enter_context(nc.semaphore(name))

    pe = Eng("pe", sem("s_pe"), is_pe=True)
    act = Eng("act", sem("s_act"))
    dve = Eng("dve", sem("s_dve"))
    pool = Eng("pool", sem("s_pool"))
    sp = Eng("sp", sem("s_sp"))
    sp.pool = [[sem("dq_sp%d" % i), 0] for i in range(16)]
    pool.pool = [[sem("dq_pl%d" % i), 0] for i in range(16)]
    engines = [pe, act, dve, pool, sp]

    st = {"off": 0}

    def alloc(shape, dt):
        n = int(np.prod(shape[1:]))
        nb = n * (4 if dt == F32 else 2)
        nb = (nb + 63) // 64 * 64
        off = st["off"]
        st["off"] += nb // 2
        assert st["off"] <= ARENA, ("arena overflow", st["off"])
        v = arena[0:shape[0], off:off + nb // 2]
        if dt == F32:
            v = v.bitcast(F32)
        v = v[:, 0:n]
        if len(shape) == 3:
            v = v.rearrange("p (a b) -> p a b", a=shape[1])
        elif len(shape) == 4:
            v = v.rearrange("p (a b c) -> p a b c", a=shape[1], b=shape[2])
        return v

    def barrier():
        toks = []
        for e in engines:
            if e.n > 0:
                toks.append((e.sem, e.n))
            if e.pool:
                for (s, v) in e.pool:
                    if v > 0:
                        toks.append((s, v))
        for e in engines:
            waits = []
            for (s, v) in toks:
                if e.is_pe and s is e.sem:
                    continue
                if e.seen.get(id(s), 0) >= v:
                    continue
                e.seen[id(s)] = v
                waits.append((s, v))

            def emit(eng, waits=waits):
                for (s, v) in waits:
                    eng.wait_ge(s, v)
            e.prog.append(emit)

    ident_b = alloc([128, 128], BF16)
    ident_f = alloc([128, 128], F32)
    identB = Buf()
    pool.op(lambda e: e.memset(ident_f, 0.0), writes=[identB])
    pool.op(lambda e: e.affine_select(out=ident_f, in_=ident_f, pattern=[[-1, 128]],
                                      compare_op=ALU.not_equal, fill=1.0, base=0,
                                      channel_multiplier=1), reads=[identB], writes=[identB])
    pool.op(lambda e: e.tensor_copy(out=ident_b, in_=ident_f), reads=[identB], writes=[identB])
    base_off = st["off"]

    def colgain(name):
        t = alloc([128, 8], F32)
        t8 = alloc([8, 128], F32)
        b = Buf(); b8 = Buf()
        sp.dma(t8, dr[name].rearrange("(c p) -> c p", p=128), writes=[b8])
        pe.op(lambda e: e.transpose(banks[0][:, 0:8], t8, ident_f[0:8, 0:8]), reads=[b8, identB], writes=[bankb[0]])
        act.op(lambda e: e.activation(out=t, in_=banks[0][:, 0:8], func=AF.Copy), reads=[bankb[0]], writes=[b])
        return t, b

    def rowgain(name):
        t = alloc([128, D], F32)
        b = Buf()
        sp.dma(t, dr[name].partition_broadcast(128), writes=[b])
        return t, b

    def rstd_ops(ss, out, n, invn, rb, wb):
        act.op(lambda e: e.activation(out=out[:, 0:n], in_=ss[:, 0:n], func=AF.Ln, scale=invn, bias=EPS),
               reads=rb, writes=wb)
        act.op(lambda e: e.activation(out=out[:, 0:n], in_=out[:, 0:n], func=AF.Exp, scale=-0.5),
               reads=wb, writes=wb)

    def load_weight(dst, src, nchunk, b):
        for c in range(nchunk):
            pool.dma(dst[:, c, :], src[c * 128:(c + 1) * 128, :], writes=[b])

    def norm_transpose(r, rB, ns, xn, xnB, xnT, xnTB, gcol, gcolB, ss, rs, sB, junk, junkB, pb):
        for s in range(ns):
            act.op(lambda e, s=s: e.activation(out=junk, in_=r[:, s, :], func=AF.Square,
                                               accum_out=ss[:, s:s + 1]),
                   reads=[rB[s]], writes=[junkB, sB])
        rstd_ops(ss, rs, ns, 1.0 / D, [sB], [sB])
        for s in range(ns):
            dve.op(lambda e, s=s: e.tensor_scalar(out=xn[:, s, :], in0=r[:, s, :], scalar1=rs[:, s:s + 1],
                                                  scalar2=None, op0=ALU.mult),
                   reads=[rB[s], sB], writes=[xnB[s]])
        for dc in range(DC):
            bi = pb[dc % len(pb)]
            pt = banks[bi][:].bitcast(BF16)

            def tr(e, dc=dc, pt=pt):
                ins = None
                for s in range(ns):
                    ins = e.transpose(pt[:, s * 128:(s + 1) * 128], xn[:, s, dc * 128:(dc + 1) * 128], ident_b)
                return ins
            pe.op(tr, reads=[xnB[s] for s in range(ns)] + [identB], writes=[bankb[bi]])
            if dc % 2 == 0:
                act.op(lambda e, dc=dc, pt=pt: e.activation(out=xnT[:, dc, 0:ns * 128], in_=pt[:, 0:ns * 128],
                                                            func=AF.Copy, scale=gcol[:, dc:dc + 1]),
                       reads=[bankb[bi], gcolB], writes=[xnTB])
            else:
                dve.op(lambda e, dc=dc, pt=pt: e.tensor_scalar(out=xnT[:, dc, 0:ns * 128], in0=pt[:, 0:ns * 128],
                                                               scalar1=gcol[:, dc:dc + 1], scalar2=None, op0=ALU.mult),
                       reads=[bankb[bi], gcolB], writes=[xnTB])

    def post_resid(pyb, r, rB, s, ysb, ysbB, junk, junkB, ss, rs, sB, grow, growB, half):
        py = banks[pyb[0]]
        py1 = banks[pyb[1]]
        act.op(lambda e: e.activation(out=ysb[:, 0:512], in_=py[:], func=AF.Copy), reads=[bankb[pyb[0]]], writes=[ysbB])
        dve.op(lambda e: e.tensor_copy(out=ysb[:, 512:1024], in_=py1[:]), reads=[bankb[pyb[1]]], writes=[ysbB])
        act.op(lambda e: e.activation(out=junk, in_=ysb, func=AF.Square, accum_out=ss[:, 0:1]),
               reads=[ysbB], writes=[junkB, sB])
        rstd_ops(ss, rs, 1, 1.0 / D, [sB], [sB])
        dve.op(lambda e: e.scalar_tensor_tensor(out=ysb, in0=ysb, scalar=rs[:, 0:1], in1=grow,
                                                op0=ALU.mult, op1=ALU.mult),
               reads=[ysbB, sB, growB], writes=[ysbB])
        dve.op(lambda e: e.scalar_tensor_tensor(out=r[:, s, :], in0=ysb, scalar=half, in1=r[:, s, :],
                                                op0=ALU.mult, op1=ALU.add),
               reads=[ysbB, rB[s]], writes=[rB[s]])

    def ffn_phase(tiles, pfx, final):
        st["off"] = base_off
        wg = alloc([128, DC, FF], BF16); wu = alloc([128, DC, FF], BF16); wd = alloc([128, FC, D], BF16)
        wgB, wuB, wdB = Buf(), Buf(), Buf()
        load_weight(wg, dr[pfx + "_wg"], DC, wgB)
        load_weight(wu, dr[pfx + "_wu"], DC, wuB)
        load_weight(wd, dr[pfx + "_wd"], FC, wdB)
        gpre, gpreB = colgain(pfx + "_pre")
        gpost, gpostB = rowgain(pfx + "_post")
        if final:
            gfin, gfinB = rowgain("final_norm")
        r = alloc([128, 4, D], F32); rB = [Buf() for _ in range(4)]
        xn = alloc([128, 4, D], BF16); xnB = [Buf() for _ in range(4)]
        xnB[1] = xnB[0]
        ysb = xn[:, 0:2, :].rearrange("p a b -> p (a b)").bitcast(F32)
        ysbB = xnB[0]
        xnT = alloc([128, DC, 512], BF16); xnTB = Buf()
        actT = alloc([128, FC, 512], BF16); actTB = Buf()
        sg = alloc([128, 512], F32); sgB = Buf()
        junk = alloc([128, D], BF16); junkB = Buf()
        ss = alloc([128, 4], F32); rs = alloc([128, 4], F32); sB = Buf()
        ss2 = alloc([128, 4], F32); rs2 = alloc([128, 4], F32); s2B = Buf()
        for ti, (src, dst, ns) in enumerate(tiles):
            N = ns * 128
            if ti == 0:
                sp.dma(r[:, 0:ns, :], src.rearrange("(s p) d -> p s d", p=128), writes=rB[0:ns])
            nxt = tiles[ti + 1] if ti + 1 < len(tiles) else None
            norm_transpose(r, rB, ns, xn, xnB, xnT, xnTB, gpre, gpreB, ss, rs, sB, junk, junkB, (0, 1, 2, 3))
            for fc in range(FC):
                bg = 2 + 2 * (fc % 2)
                bu = bg + 1

                def mm(e, fc=fc, bg=bg, bu=bu, N=N):
                    ins = None
                    for dc in range(DC):
                        ins = e.matmul(banks[bg][:, 0:N], lhsT=wg[:, dc, fc * 128:(fc + 1) * 128], rhs=xnT[:, dc, 0:N],
                                       start=(dc == 0), stop=(dc == DC - 1))
                    for dc in range(DC):
                        ins = e.matmul(banks[bu][:, 0:N], lhsT=wu[:, dc, fc * 128:(fc + 1) * 128], rhs=xnT[:, dc, 0:N],
                                       start=(dc == 0), stop=(dc == DC - 1))
                    return ins
                pe.op(mm, reads=[wgB, wuB, xnTB], writes=[bankb[bg], bankb[bu]])
                act.op(lambda e, bg=bg, N=N: e.activation(out=sg[:, 0:N], in_=banks[bg][:, 0:N], func=AF.Silu),
                       reads=[bankb[bg]], writes=[sgB])
                dve.op(lambda e, fc=fc, bu=bu, N=N: e.tensor_tensor(out=actT[:, fc, 0:N], in0=sg[:, 0:N],
                                                                   in1=banks[bu][:, 0:N], op=ALU.mult),
                       reads=[sgB, bankb[bu]], writes=[actTB])
            for s in range(ns):
                pyb = (6, 7)

                def dn(e, s=s):
                    ins = None
                    for hf in range(2):
                        for fc in range(FC):
                            ins = e.matmul(banks[6 + hf][:], lhsT=actT[:, fc, s * 128:(s + 1) * 128],
                                           rhs=wd[:, fc, hf * 512:(hf + 1) * 512], start=(fc == 0), stop=(fc == FC - 1))
                    return ins
                pe.op(dn, reads=[actTB, wdB], writes=[bankb[6], bankb[7]])
                post_resid(pyb, r, rB, s, ysb, ysbB, junk, junkB, ss2, rs2, s2B, gpost, gpostB, 0.5)
                if final:
                    act.op(lambda e, s=s: e.activation(out=junk, in_=r[:, s, :], func=AF.Square, accum_out=ss2[:, 1:2]),
                           reads=[rB[s]], writes=[junkB, s2B])
                    act.op(lambda e: e.activation(out=rs2[:, 1:2], in_=ss2[:, 1:2], func=AF.Ln, scale=1.0 / D, bias=EPS),
                           reads=[s2B], writes=[s2B])
                    act.op(lambda e: e.activation(out=rs2[:, 1:2], in_=rs2[:, 1:2], func=AF.Exp, scale=-0.5),
                           reads=[s2B], writes=[s2B])
                    dve.op(lambda e, s=s: e.scalar_tensor_tensor(out=r[:, s, :], in0=r[:, s, :], scalar=rs2[:, 1:2],
                                                                 in1=gfin, op0=ALU.mult, op1=ALU.mult),
                           reads=[rB[s], s2B, gfinB], writes=[rB[s]])
                sp.dma(dst[s * 128:(s + 1) * 128, :], r[:, s, :], reads=[rB[s]])
                if nxt is not None and s < nxt[2]:
                    sp.dma(r[:, s, :], nxt[0][s * 128:(s + 1) * 128, :], writes=[rB[s]])
            if nxt is not None:
                for s in range(ns, nxt[2]):
                    sp.dma(r[:, s, :], nxt[0][s * 128:(s + 1) * 128, :], writes=[rB[s]])
        barrier()

    def proj_phase():
        st["off"] = base_off
        win = alloc([128, DC, 2304], BF16); winB = Buf()
        wi = dr["w_in"]
        for c in range(DC):
            rows = wi[c * 128:(c + 1) * 128, :]
            pool.dma(win[:, c, 0:1536], rows[:, 0:1536], writes=[winB])
            srcq = rows[:, 1536:2048].rearrange("p (g j d) -> p j g d", g=2, j=4)
            for j in range(4):
                pool.dma(win[:, c, 1536 + j * 128:1536 + (j + 1) * 128].rearrange("p (g d) -> p g d", g=2),
                         srcq[:, j], writes=[winB])
            pool.dma(win[:, c, 2048:2304], rows[:, 2048:2304], writes=[winB])
        gmix, gmixB = colgain("mix_pre")
        gqk = alloc([128, 10, 64], F32); gqkB = Buf()
        g2 = alloc([128, 2, 64], F32); g2B = Buf()
        sp.dma(g2[:, 0, :], dr["q_norm"].partition_broadcast(128), writes=[g2B])
        sp.dma(g2[:, 1, :], dr["k_norm"].partition_broadcast(128), writes=[g2B])
        dve.op(lambda e: e.tensor_copy(out=gqk[:, 0:8, :], in_=g2[:, 0:1, :].broadcast_to([128, 8, 64])), reads=[g2B], writes=[gqkB])
        dve.op(lambda e: e.tensor_copy(out=gqk[:, 8:10, :], in_=g2[:, 1:2, :].broadcast_to([128, 2, 64])), reads=[g2B], writes=[gqkB])
        act.op(lambda e: e.activation(out=gqk[:, 0:8, :], in_=gqk[:, 0:8, :], func=AF.Copy, scale=0.125),
               reads=[gqkB], writes=[gqkB])
        r = alloc([128, 4, D], F32); rB = [Buf() for _ in range(4)]
        xn = alloc([128, 4, D], BF16); xnB = [Buf() for _ in range(4)]
        xnT = alloc([128, DC, 512], BF16); xnTB = Buf()
        junk = alloc([128, D], BF16); junkB = Buf()
        ss = alloc([128, 4], F32); rs = alloc([128, 4], F32); sB = Buf()
        rope = alloc([128, 4, 128], F32); ropeB = Buf()
        stq = alloc([128, 8, 512], BF16); stqB = Buf()
        vast = alloc([128, 4, 8, 80], BF16); vastB = Buf()
        vbst = alloc([128, 4, 2, 80], BF16); vbstB = Buf()
        qbst = alloc([128, 4, 512], BF16); qbstB = Buf()
        kbst = alloc([128, 512], BF16); kbstB = Buf()
        qk = alloc([128, 4, 10, 64], F32); qkB = Buf()
        t1 = alloc([128, 4, 10, 64], F32); t1B = Buf()
        t2 = alloc([128, 4, 10, 64], F32); t2B = Buf()
        qr = alloc([128, 4, 10, 64], BF16); qrB = Buf()
        ssq = alloc([128, 4, 10], F32); rq = alloc([128, 4, 10], F32); sqB = Buf()
        pool.op(lambda e: e.memset(vast, 1.0), writes=[vastB])
        pool.op(lambda e: e.memset(vbst, 1.0), writes=[vbstB])
        KP2 = int(os.environ.get("KP2", "9"))

        tiles = []
        for t in range(14):
            ns = 4 if t < 13 else 1
            tiles.append(("own", t * 512, ns))
        for t in range(NSF // 512):
            tiles.append(("full", t * 512, 4))
        if os.environ.get("KSMALL"):
            tiles = tiles[:15]
        if KP2 < 2:
            tiles = []
        r2 = [r, alloc([128, 4, D], F32)]; r2B = [rB, [Buf() for _ in range(4)]]
        rope2 = [rope, alloc([128, 4, 128], F32)]; rope2B = [ropeB, Buf()]

        def p2_load(tile, bi):
            kind_, t0_, ns_ = tile
            hs = dr["h1_own"] if kind_ == "own" else dr["h1_full"]
            rsrc_ = dr["rope_own"] if kind_ == "own" else dr["rope_full"]
            sp.dma(r2[bi][:, 0:ns_, :], hs[t0_:t0_ + ns_ * 128, :].rearrange("(s p) d -> p s d", p=128), writes=r2B[bi][0:ns_])
            sp.dma(rope2[bi][:, 0:ns_, :], rsrc_[t0_:t0_ + ns_ * 128, :].rearrange("(s p) d -> p s d", p=128), writes=[rope2B[bi]])
        for tix, (kind, t0, ns) in enumerate(tiles):
            N = ns * 128
            full = kind == "own"
            hsrc = dr["h1_own"] if full else dr["h1_full"]
            rsrc = dr["rope_own"] if full else dr["rope_full"]
            r = r2[tix % 2]; rB = r2B[tix % 2]; rope = rope2[tix % 2]; ropeB = rope2B[tix % 2]
            if tix == 0:
                p2_load(tiles[0], 0)
            if tix + 1 < len(tiles):
                p2_load(tiles[tix + 1], (tix + 1) % 2)
            norm_transpose(r, rB, ns, xn, xnB, xnT, xnTB, gmix, gmixB, ss, rs, sB, junk, junkB, (0, 1))
            if KP2 < 3:
                continue
            if full:
                for p in range(8):
                    bi = 2 + (p % 2)
                    col = (0 if p < 4 else 512) + (p % 4) * 128

                    def mmq(e, bi=bi, col=col, N=N):
                        ins = None
                        for dc in range(DC):
                            ins = e.matmul(banks[bi][:, 0:N], lhsT=win[:, dc, col:col + 128], rhs=xnT[:, dc, 0:N],
                                           start=(dc == 0), stop=(dc == DC - 1))
                        return ins
                    pe.op(mmq, reads=[winB, xnTB], writes=[bankb[bi]])
                    if p % 2 == 0:
                        act.op(lambda e, p=p, bi=bi, N=N: e.activation(out=stq[:, p, 0:N], in_=banks[bi][:, 0:N], func=AF.Copy),
                               reads=[bankb[bi]], writes=[stqB])
                    else:
                        dve.op(lambda e, p=p, bi=bi, N=N: e.tensor_copy(out=stq[:, p, 0:N], in_=banks[bi][:, 0:N]),
                               reads=[bankb[bi]], writes=[stqB])
                if t0 < NQ:
                    sp.dma(dr["qaT"][:, :, t0:t0 + N], stq[:, 0:4, 0:N], reads=[stqB])
                sp.dma(dr["kaT"][:, :, t0:t0 + N], stq[:, 4:8, 0:N], reads=[stqB])
            for s in range(ns):
                bva, bqb, bkv = (4, 5, 6) if s % 2 == 0 else (2, 3, 7)
                if full:
                    def mmv(e, s=s, bva=bva):
                        ins = None
                        for dc in range(DC):
                            ins = e.matmul(banks[bva][:], lhsT=xnT[:, dc, s * 128:(s + 1) * 128], rhs=win[:, dc, 1024:1536],
                                           start=(dc == 0), stop=(dc == DC - 1))
                        return ins
                    pe.op(mmv, reads=[winB, xnTB], writes=[bankb[bva]])
                    act.op(lambda e, s=s, bva=bva: e.activation(out=vast[:, s, :, 0:64],
                                                                in_=banks[bva][:].rearrange("p (h d) -> p h d", h=8), func=AF.Copy),
                           reads=[bankb[bva]], writes=[vastB])

                    def mmqb(e, s=s, bqb=bqb):
                        ins = None
                        for dc in range(DC):
                            ins = e.matmul(banks[bqb][:], lhsT=xnT[:, dc, s * 128:(s + 1) * 128], rhs=win[:, dc, 1536:2048],
                                           start=(dc == 0), stop=(dc == DC - 1))
                        return ins
                    pe.op(mmqb, reads=[winB, xnTB], writes=[bankb[bqb]])
                    act.op(lambda e, s=s, bqb=bqb: e.activation(out=qk[:, s, 0:8, :], in_=banks[bqb][:].rearrange("p (h d) -> p h d", h=8),
                                                                func=AF.Copy), reads=[bankb[bqb]], writes=[qkB])

                def mmkv(e, s=s, bkv=bkv):
                    ins = None
                    for dc in range(DC):
                        ins = e.matmul(banks[bkv][:, 0:256], lhsT=xnT[:, dc, s * 128:(s + 1) * 128], rhs=win[:, dc, 2048:2304],
                                       start=(dc == 0), stop=(dc == DC - 1))
                    return ins
                pe.op(mmkv, reads=[winB, xnTB], writes=[bankb[bkv]])
                dve.op(lambda e, s=s, bkv=bkv: e.tensor_copy(out=qk[:, s, 8:10, :], in_=banks[bkv][:, 0:128].rearrange("p (h d) -> p h d", h=2)),
                       reads=[bankb[bkv]], writes=[qkB])
                dve.op(lambda e, s=s, bkv=bkv: e.tensor_copy(out=vbst[:, s, :, 0:64],
                                                             in_=banks[bkv][:, 128:256].rearrange("p (h d) -> p h d", h=2)),
                       reads=[bankb[bkv]], writes=[vbstB])
            h0 = 0 if full else 8
            nh = 10 - h0
            Q = qk[:, 0:ns, h0:10, :]
            T1 = t1[:, 0:ns, h0:10, :]
            dve.op(lambda e, Q=Q, T1=T1: e.tensor_tensor(out=T1, in0=Q, in1=Q, op=ALU.mult), reads=[qkB], writes=[t1B])
            dve.op(lambda e, T1=T1, h0=h0, ns=ns: e.tensor_reduce(out=ssq[:, 0:ns, h0:10], in_=T1, axis=AX.X, op=ALU.add),
                   reads=[t1B], writes=[sqB])
            act.op(lambda e, h0=h0, ns=ns: e.activation(out=rq[:, 0:ns, h0:10], in_=ssq[:, 0:ns, h0:10], func=AF.Ln, scale=1.0 / 64, bias=EPS),
                   reads=[sqB], writes=[sqB])
            act.op(lambda e, h0=h0, ns=ns: e.activation(out=rq[:, 0:ns, h0:10], in_=rq[:, 0:ns, h0:10], func=AF.Exp, scale=-0.5),
                   reads=[sqB], writes=[sqB])
            for s in range(ns):
                dve.op(lambda e, s=s, h0=h0, nh=nh: e.tensor_tensor(out=t1[:, s, h0:10, :], in0=qk[:, s, h0:10, :],
                                                                   in1=rq[:, s, h0:10].unsqueeze(2).broadcast_to([128, nh, 64]), op=ALU.mult),
                       reads=[qkB, sqB, t1B], writes=[t1B])
            dve.op(lambda e, Q=Q, T1=T1, h0=h0, ns=ns, nh=nh: e.tensor_tensor(
                out=Q, in0=T1, in1=gqk[:, h0:10, :].unsqueeze(1).broadcast_to([128, ns, nh, 64]), op=ALU.mult),
                reads=[t1B, gqkB], writes=[qkB])
            Cb = rope[:, 0:ns, 0:64].unsqueeze(2).broadcast_to([128, ns, nh, 64])
            dve.op(lambda e, Q=Q, T1=T1, Cb=Cb: e.tensor_tensor(out=T1, in0=Q, in1=Cb, op=ALU.mult),
                   reads=[qkB, ropeB], writes=[t1B])
            for hf in range(2):
                for xi in range(2):
                    lo = hf * 32 + xi * 16
                    so = hf * 32 + (1 - xi) * 16
                    Sb = rope[:, 0:ns, 64 + lo:64 + lo + 16].unsqueeze(2).broadcast_to([128, ns, nh, 16])
                    dve.op(lambda e, h0=h0, lo=lo, so=so, Sb=Sb, ns=ns: e.tensor_tensor(
                        out=t2[:, 0:ns, h0:10, lo:lo + 16], in0=qk[:, 0:ns, h0:10, so:so + 16], in1=Sb, op=ALU.mult),
                        reads=[qkB, ropeB], writes=[t2B])
            dve.op(lambda e, h0=h0, ns=ns, T1=T1: e.tensor_tensor(out=qr[:, 0:ns, h0:10, :], in0=T1, in1=t2[:, 0:ns, h0:10, :], op=ALU.add),
                   reads=[t1B, t2B], writes=[qrB])
            for s in range(ns):
                qrf = qr[:, s].rearrange("p h d -> p (h d)")
                pbi = 7 if s % 2 == 0 else 4
                ptb = banks[pbi][:].bitcast(BF16)
                if full:
                    def trq(e, qrf=qrf, ptb=ptb):
                        ins = None
                        for j in range(4):
                            ins = e.transpose(ptb[:, j * 128:(j + 1) * 128], qrf[:, j * 128:(j + 1) * 128], ident_b)
                        ins = e.transpose(ptb[:, 512:640], qrf[:, 512:640], ident_b)
                        return ins
                    pe.op(trq, reads=[qrB, identB], writes=[bankb[pbi]])
                    act.op(lambda e, s=s, ptb=ptb: e.activation(out=qbst[:, :, s * 128:(s + 1) * 128],
                                                                in_=ptb[:, 0:512].rearrange("p (j t) -> p j t", j=4), func=AF.Copy),
                           reads=[bankb[pbi]], writes=[qbstB])
                else:
                    pe.op(lambda e, qrf=qrf, ptb=ptb: e.transpose(ptb[:, 512:640], qrf[:, 512:640], ident_b),
                          reads=[qrB, identB], writes=[bankb[pbi]])
                act.op(lambda e, s=s, ptb=ptb: e.activation(out=kbst[:, s * 128:(s + 1) * 128], in_=ptb[:, 512:640], func=AF.Copy),
                       reads=[bankb[pbi]], writes=[kbstB])
            if KP2 < 7:
                continue
            KST = int(os.environ.get("KST", "63"))
            if full:
                if KST & 1:
                    sp.dma(dr["va"][t0:t0 + N, :].rearrange("(s p) c -> p s c", p=128),
                           vast[:, 0:ns].rearrange("p s h d -> p s (h d)"), reads=[vastB])
                if t0 < NQ and (KST & 2):
                    sp.dma(dr["qbT"][:, :, t0:t0 + N], qbst[:, :, 0:N], reads=[qbstB])
                if KST & 4:
                    KBV = int(os.environ.get("KBV", "0"))
                    if KBV == 1:
                        pool.dma(dr["kbT_own"][:, t0:t0 + N], kbst[:, 0:N], reads=[kbstB])
                    elif KBV == 2:
                        if N == 512:
                            sp.dma(dr["kbT_own"][:, t0:t0 + N], kbst[:, 0:N], reads=[kbstB])
                    elif KBV == 4:
                        sp.dma(dr["kbT_own"][:, t0:t0 + N], kbst[:, 0:N], reads=[])
                    elif KBV == 5:
                        if t0 != 6656:
                            sp.dma(dr["kbT_own"][:, t0:t0 + N], kbst[:, 0:N], reads=[kbstB])
                    elif KBV == 3:
                        sp.dma(dr["kbT_own"][:, t0:t0 + N].unsqueeze(1), kbst[:, 0:N].unsqueeze(1), reads=[kbstB])
                    else:
                        sp.dma(dr["kbT_own"][:, t0:t0 + N], kbst[:, 0:N], reads=[kbstB])
                if KST & 8:
                    sp.dma(dr["vb_own"][t0:t0 + N, :].rearrange("(s p) c -> p s c", p=128),
                           vbst[:, 0:ns].rearrange("p s h d -> p s (h d)"), reads=[vbstB])
            else:
                if KST & 16:
                    sp.dma(dr["kbT_full"][:, t0:t0 + N], kbst[:, 0:N], reads=[kbstB])
                if KST & 32:
                    sp.dma(dr["vb_full"][t0:t0 + N, :].rearrange("(s p) c -> p s c", p=128),
                           vbst[:, 0:ns].rearrange("p s h d -> p s (h d)"), reads=[vbstB])
        barrier()

    def attn_phase():
        st["off"] = base_off
        KTM = 129
        kT = alloc([128, 128 * 128 + 16], BF16); kTB = Buf()
        vS = alloc([128, KTM, 160], BF16); vSB = Buf()
        kaM = alloc([128, 4, 16], BF16); vaM = alloc([16, 640], BF16); mbT = alloc([16, 8], F32); metaB = Buf()
        sp.dma(kaM, dr["kaT"][:, :, NQ:NQ + 16], writes=[metaB])
        sp.dma(vaM, dr["va"][NQ:NQ + 16, :], writes=[metaB])
        mb8 = alloc([8, 16], F32); mb8B = Buf()
        sp.dma(mb8, dr["meta_bias"], writes=[mb8B])
        pe.op(lambda e: e.transpose(banks[0][0:16, 0:8], mb8, ident_f[0:8, 0:8]), reads=[mb8B, identB], writes=[bankb[0]])
        act.op(lambda e: e.activation(out=mbT, in_=banks[0][0:16, 0:8], func=AF.Copy), reads=[bankb[0]], writes=[metaB])
        qa2 = [alloc([128, 4, 256], BF16) for _ in range(2)]; qa2B = [Buf(), Buf()]
        kaw2 = [alloc([128, 4, 768], BF16) for _ in range(2)]; kaw2B = [Buf(), Buf()]
        vaw2 = [alloc([128, 6, 640], BF16) for _ in range(2)]; vaw2B = [Buf(), Buf()]
        qb2 = [alloc([128, 4, 256], BF16) for _ in range(2)]; qb2B = [Buf(), Buf()]
        tab = [alloc([128, 6, 2, 256], F32) for _ in range(2)]; tabB = [Buf(), Buf()]
        tmp = [alloc([128, 512], F32) for _ in range(4)]; tmpB = [Buf() for _ in range(4)]
        pn = [alloc([128, 512], BF16) for _ in range(4)]; pnB = [Buf() for _ in range(4)]
        pg2 = [alloc([128, 1024], BF16) for _ in range(3)]; pg2B = [Buf() for _ in range(3)]
        oa2 = [alloc([65, 8, 256], F32) for _ in range(2)]; oa2B = [Buf(), Buf()]
        ob = alloc([65, 2, 2, 512], F32); obB = Buf()
        yc = alloc([128, 16, 64], F32); ycB = Buf()
        rc = alloc([128, 16], F32); rcB = Buf()
        yo = alloc([128, D], BF16); yoB = Buf()
        junk = alloc([128, 512], BF16); junkB = Buf()
        ss = alloc([128, 2], F32); rs = alloc([128, 2], F32); sB = Buf()
        cnt = {"s": 0, "g": 0, "blk": 0}

        def run_seq(segs, blocks):
            nkt = sum(n for (_k, _v, n) in segs) // 128
            NK = nkt * 128
            base = 0
            for (kt_src, v_src, n) in segs:
                for c in range(0, n, 2048):
                    sp.dma(kT[:, base + c:base + c + 2048], kt_src[:, c:c + 2048], writes=[kTB])
                for c in range(0, n // 128, 16):
                    sp.dma(vS[:, base // 128 + c:base // 128 + c + 16, :],
                           v_src[c * 128:(c + 16) * 128, :].rearrange("(t p) c -> p t c", p=128), writes=[vSB])
                base += n
            sp.dma(kT[:, NK:NK + 16], dr["kbT_own"][:, NQ:NQ + 16], writes=[kTB])
            sp.dma(vS[0:16, nkt, :], dr["vb_own"][NQ:NQ + 16, :], writes=[vSB])
            def blk_load(blk, bi):
                q0_, w0_, _t = blk
                sp.dma(qa2[bi], dr["qaT"][:, :, q0_:q0_ + 256], writes=[qa2B[bi]])
                sp.dma(kaw2[bi], dr["kaT"][:, :, w0_:w0_ + 768], writes=[kaw2B[bi]])
                sp.dma(vaw2[bi], dr["va"][w0_:w0_ + 768, :].rearrange("(t p) c -> p t c", p=128), writes=[vaw2B[bi]])
                sp.dma(qb2[bi], dr["qbT"][:, :, q0_:q0_ + 256], writes=[qb2B[bi]])
            def do_block(q0, w0, typ, qa, qaB, kaw, kawB, vaw, vawB, qb, qbB, oa, oaB):
                steps = [(p, o) for p in range(4) for o in range(7)]

                def na_front(i):
                    p, o = steps[i]
                    tb = tab[p % 2]; tbB = tabB[p % 2]
                    if o == 0:
                        for e2 in range(2):
                            sp.dma(tb[:, :, e2, :], dr["na_tab"][typ, 2 * p + e2].rearrange("k (o q) -> k o q", o=6), writes=[tbB])
                    j = cnt["g"] % 2; cnt["g"] += 1
                    k = i % 4
                    sb2 = [bankb[2 * j], bankb[2 * j + 1]]
                    if o < 6:
                        def qk(e):
                            ins = None
                            for e2 in range(2):
                                ins = e.matmul(banks[2 * j + e2][:, 0:256],
                                               lhsT=kaw[64 * e2:64 * e2 + 64, p, o * 128:(o + 1) * 128],
                                               rhs=qa[64 * e2:64 * e2 + 64, p, :], start=True, stop=True)
                            return ins
                        pe.op(qk, reads=[kawB, qaB], writes=sb2)
                        dve.op(lambda e: e.scalar_tensor_tensor(
                            out=tmp[k].rearrange("k (e q) -> k e q", e=2),
                            in0=big[j][:, :].rearrange("k (e q) -> k e q", e=2)[:, :, 0:256], scalar=0.125, in1=tb[:, o],
                            op0=ALU.mult, op1=ALU.add), reads=sb2 + [tbB], writes=[tmpB[k]])
                        act.op(lambda e: e.activation(out=pn[k], in_=tmp[k], func=AF.Exp),
                               reads=[tmpB[k]], writes=[pnB[k]])
                    else:
                        def qkm(e):
                            ins = None
                            for e2 in range(2):
                                ins = e.matmul(banks[2 * j + e2][0:16, 0:256], lhsT=kaM[64 * e2:64 * e2 + 64, p, :],
                                               rhs=qa[64 * e2:64 * e2 + 64, p, :], start=True, stop=True)
                            return ins
                        pe.op(qkm, reads=[metaB, qaB], writes=sb2)
                        for e2 in range(2):
                            h = 2 * p + e2
                            act.op(lambda e, e2=e2, h=h: e.activation(
                                out=pn[k][0:16, e2 * 256:(e2 + 1) * 256], in_=banks[2 * j + e2][0:16, 0:256],
                                func=AF.Exp, scale=0.125, bias=mbT[:, h:h + 1]), reads=[bankb[2 * j + e2], metaB], writes=[pnB[k]])

                def na_back(i):
                    p, o = steps[i]
                    k = i % 4

                    def pv(e):
                        ins = None
                        for e2 in range(2):
                            h = 2 * p + e2
                            if o < 6:
                                ins = e.matmul(banks[4 + e2][0:65, 0:256], lhsT=vaw[:, o, h * 80:h * 80 + 65],
                                               rhs=pn[k][:, e2 * 256:(e2 + 1) * 256], start=(o == 0), stop=False)
                            else:
                                ins = e.matmul(banks[4 + e2][0:65, 0:256], lhsT=vaM[:, h * 80:h * 80 + 65],
                                               rhs=pn[k][0:16, e2 * 256:(e2 + 1) * 256], start=False, stop=True)
                        return ins
                    pe.op(pv, reads=[vawB, metaB, pnB[k]], writes=[bankb[4], bankb[5]])
                    if o == 6:
                        act.op(lambda e: e.activation(out=oa[:, 2 * p, :], in_=banks[4][0:65, 0:256], func=AF.Copy),
                               reads=[bankb[4]], writes=[oaB])
                        dve.op(lambda e: e.tensor_copy(out=oa[:, 2 * p + 1, :], in_=banks[5][0:65, 0:256]),
                               reads=[bankb[5]], writes=[oaB])
                LN = 3
                for t in range(len(steps) + LN):
                    if t < len(steps):
                        na_front(t)
                    if t >= LN:
                        na_back(t - LN)
                if cnt.get("pending"):
                    cnt["pending"]()
                    cnt["pending"] = None
                units = [(qt, kt) for qt in range(2) for kt in range(nkt + 1)]

                def g_front(i):
                    qt, kt = units[i]
                    j = cnt["g"] % 2; cnt["g"] += 1
                    k = i % 3
                    kk = 128 if kt < nkt else 16

                    def qk(e):
                        ins = None
                        for g in range(2):
                            ins = e.matmul(banks[2 * j + g][0:kk, :], lhsT=kT[64 * g:64 * g + 64, kt * 128:kt * 128 + kk],
                                           rhs=qb[64 * g:64 * g + 64, :, qt * 128:(qt + 1) * 128], start=True, stop=True)
                        return ins
                    pe.op(qk, reads=[kTB, qbB], writes=[bankb[2 * j], bankb[2 * j + 1]])
                    act.op(lambda e: e.activation(out=pg2[k][0:kk, :], in_=big[j][0:kk, :], func=AF.Exp),
                           reads=[bankb[2 * j], bankb[2 * j + 1]], writes=[pg2B[k]])

                def g_back(i):
                    qt, kt = units[i]
                    k = i % 3
                    kk = 128 if kt < nkt else 16

                    def pv(e):
                        ins = None
                        for g in range(2):
                            ins = e.matmul(banks[4 + g][0:65, :], lhsT=vS[0:kk, kt, g * 80:g * 80 + 65],
                                           rhs=pg2[k][0:kk, g * 512:(g + 1) * 512],
                                           start=(kt == 0), stop=(kt == nkt))
                        return ins
                    pe.op(pv, reads=[vSB, pg2B[k]], writes=[bankb[4], bankb[5]])
                    if kt == nkt:
                        dve.op(lambda e: e.tensor_copy(out=ob[:, 0, qt, :], in_=banks[4][0:65, :]),
                               reads=[bankb[4]], writes=[obB])
                        act.op(lambda e: e.activation(out=ob[:, 1, qt, :], in_=banks[5][0:65, :], func=AF.Copy),
                               reads=[bankb[5]], writes=[obB])
                LG = 1
                for t in range(len(units) + LG):
                    if t < len(units):
                        g_front(t)
                    if t >= LG:
                        g_back(t - LG)
                def fin():
                    for qt in range(2):
                        for grp in range(4):
                            bi = 6 + (grp % 2)
                            pt = banks[bi][:, 0:260].rearrange("p (h c) -> p h c", h=4)

                            def trf(e, grp=grp, qt=qt, pt=pt):
                                ins = None
                                for j in range(4):
                                    if grp < 2:
                                        src = oa[:, grp * 4 + j, qt * 128:(qt + 1) * 128]
                                    else:
                                        src = ob[:, grp - 2, qt, j * 128:(j + 1) * 128]
                                    ins = e.transpose(pt[:, j, :], src, ident_f[0:65, 0:65])
                                return ins
                            pe.op(trf, reads=[oaB, obB, identB], writes=[bankb[bi]])
                            dve.op(lambda e, grp=grp, pt=pt: e.reciprocal(out=rc[:, grp * 4:(grp + 1) * 4], in_=pt[:, :, 64]),
                                   reads=[bankb[bi]], writes=[rcB])
                            dve.op(lambda e, grp=grp, pt=pt: e.tensor_tensor(
                                out=yc[:, grp * 4:(grp + 1) * 4, :], in0=pt[:, :, 0:64],
                                in1=rc[:, grp * 4:(grp + 1) * 4].unsqueeze(2).broadcast_to([128, 4, 64]), op=ALU.mult),
                                reads=[bankb[bi], rcB], writes=[ycB])
                        ycf = yc.rearrange("p h d -> p (h d)")
                        for a in range(2):
                            act.op(lambda e, a=a, ycf=ycf: e.activation(out=junk, in_=ycf[:, a * 512:(a + 1) * 512], func=AF.Square,
                                                                        accum_out=ss[:, a:a + 1]), reads=[ycB], writes=[junkB, sB])
                        rstd_ops(ss, rs, 2, 1.0 / 512, [sB], [sB])
                        for a in range(2):
                            dve.op(lambda e, a=a, ycf=ycf: e.tensor_scalar(out=yo[:, a * 512:(a + 1) * 512], in0=ycf[:, a * 512:(a + 1) * 512],
                                                                           scalar1=rs[:, a:a + 1], scalar2=None, op0=ALU.mult),
                                   reads=[ycB, sB], writes=[yoB])
                        sp.dma(dr["ycat"][q0 + qt * 128:q0 + (qt + 1) * 128, :], yo, reads=[yoB])
                cnt["pending"] = fin


            for bix, blk in enumerate(blocks):
                if bix == 0:
                    blk_load(blocks[0], cnt["blk"] % 2)
                bi = cnt["blk"] % 2; cnt["blk"] += 1
                if bix + 1 < len(blocks):
                    blk_load(blocks[bix + 1], cnt["blk"] % 2)
                do_block(blk[0], blk[1], blk[2], qa2[bi], qa2B[bi], kaw2[bi], kaw2B[bi],
                         vaw2[bi], vaw2B[bi], qb2[bi], qb2B[bi], oa2[bi], oa2B[bi])

        blocks = []
        for b in range(16):
            R = 4 * b
            Rw = min(max(R - 4, 0), 52)
            typ = 1 if b == 0 else (2 if b == 15 else 0)
            blocks.append((R * 64, Rw * 64, typ))
        if os.environ.get("KSMALL"):
            blocks = blocks[:2]
        seg_p = [(dr["kbT_own"][:, 0:NPR], dr["vb_own"][0:NPR, :], NPR)]
        run_seq(seg_p, blocks)
        blocks = []
        for b in range(8):
            Rl = 4 + 4 * b
            typ = 3 if b == 0 else (4 if b == 7 else 0)
            blocks.append((NPR + Rl * 64, NPR + (Rl - 4) * 64, typ))
        o0 = NPR + 256
        seg_s = [(dr["kbT_full"], dr["vb_full"], NSF),
                 (dr["kbT_own"][:, o0:o0 + 2048], dr["vb_own"][o0:o0 + 2048, :], 2048)]
        if os.environ.get("KSMALL"):
            run_seq(seg_p, blocks[:1])
        else:
            run_seq(seg_s, blocks)
        if cnt.get("pending"):
            cnt["pending"]()
            cnt["pending"] = None
        barrier()

    def wout_phase(qtiles):
        st["off"] = base_off
        wo = alloc([128, DC, D], BF16); woB = Buf()
        load_weight(wo, dr["w_out"], DC, woB)
        ggrp, ggrpB = colgain("grp_ab")
        gpost, gpostB = rowgain("mix_post")
        r = alloc([128, 4, D], F32); rB = [Buf() for _ in range(4)]
        yt = alloc([128, 4, D], BF16); ytB = [Buf() for _ in range(4)]
        ycT = alloc([128, DC, 512], BF16); ycTB = Buf()
        ysb = alloc([128, D], F32); ysbB = Buf()
        junk = alloc([128, D], BF16); junkB = Buf()
        ss = alloc([128, 4], F32); rs = alloc([128, 4], F32); sB = Buf()
        rr = [r, alloc([128, 4, D], F32)]; rrB = [rB, [Buf() for _ in range(4)]]
        yy = [yt, alloc([128, 4, D], BF16)]; yyB = [ytB, [Buf() for _ in range(4)]]

        def wo_load(t0_, bi):
            sp.dma(rr[bi], dr["h1_own"][t0_:t0_ + 512, :].rearrange("(s p) d -> p s d", p=128), writes=rrB[bi])
            sp.dma(yy[bi], dr["ycat"][t0_:t0_ + 512, :].rearrange("(s p) d -> p s d", p=128), writes=yyB[bi])
        for tix, t0 in enumerate(qtiles):
            if tix == 0:
                wo_load(qtiles[0], 0)
            if tix + 1 < len(qtiles):
                wo_load(qtiles[tix + 1], (tix + 1) % 2)
            r, rB, yt, ytB = rr[tix % 2], rrB[tix % 2], yy[tix % 2], yyB[tix % 2]
            for dc in range(DC):
                bi = dc % 2
                pt = banks[bi][:].bitcast(BF16)

                def tr(e, dc=dc, pt=pt, yt=yt):
                    ins = None
                    for s in range(4):
                        ins = e.transpose(pt[:, s * 128:(s + 1) * 128], yt[:, s, dc * 128:(dc + 1) * 128], ident_b)
                    return ins
                pe.op(tr, reads=ytB + [identB], writes=[bankb[bi]])
                if dc % 2 == 0:
                    act.op(lambda e, dc=dc, pt=pt: e.activation(out=ycT[:, dc, :], in_=pt[:, 0:512], func=AF.Copy,
                                                                scale=ggrp[:, dc:dc + 1]), reads=[bankb[bi], ggrpB], writes=[ycTB])
                else:
                    dve.op(lambda e, dc=dc, pt=pt: e.tensor_scalar(out=ycT[:, dc, :], in0=pt[:, 0:512], scalar1=ggrp[:, dc:dc + 1],
                                                                   scalar2=None, op0=ALU.mult), reads=[bankb[bi], ggrpB], writes=[ycTB])
            for s in range(4):
                def mo(e, s=s):
                    ins = None
                    for hf in range(2):
                        for dc in range(DC):
                            ins = e.matmul(banks[6 + hf][:], lhsT=ycT[:, dc, s * 128:(s + 1) * 128],
                                           rhs=wo[:, dc, hf * 512:(hf + 1) * 512], start=(dc == 0), stop=(dc == DC - 1))
                    return ins
                pe.op(mo, reads=[ycTB, woB], writes=[bankb[6], bankb[7]])
                post_resid((6, 7), r, rB, s, ysb, ysbB, junk, junkB, ss, rs, sB, gpost, gpostB, 1.0)
            sp.dma(dr["h2"][t0:t0 + 512, :].rearrange("(s p) d -> p s d", p=128), r, reads=rB)
        barrier()

    t1 = []
    for t in range(14):
        ns = 4 if t < 13 else 1
        t1.append((dr["x_own"][t * 512:t * 512 + ns * 128, :], dr["h1_own"][t * 512:t * 512 + ns * 128, :], ns))
    for t in range(NSF // 512):
        t1.append((dr["xs_full"][t * 512:(t + 1) * 512, :], dr["h1_full"][t * 512:(t + 1) * 512, :], 4))
    STOP = int(os.environ.get("KSTOP", "9"))
    if os.environ.get("KSMALL"):
        t1 = t1[:15]
    ffn_phase(t1, "ffn1", False)
    if STOP >= 2:
        proj_phase()
    if STOP >= 3:
        attn_phase()
    qtiles = [t * 512 for t in range(8)] + [NPR + 256 + t * 512 for t in range(4)]
    if os.environ.get("KSMALL"):
        qtiles = qtiles[:1]
    if STOP >= 4:
        wout_phase(qtiles)
    t4 = []
    for t in range(8):
        t4.append((dr["h2"][t * 512:(t + 1) * 512, :], y_p[t * 512:(t + 1) * 512, :], 4))
    for t in range(4):
        a = NPR + 256 + t * 512
        t4.append((dr["h2"][a:a + 512, :], y_s[t * 512:(t + 1) * 512, :], 4))
    if os.environ.get("KSMALL"):
        t4 = t4[:1]
    if STOP >= 5:
        ffn_phase(t4, "ffn2", True)

    with nc.allow_non_contiguous_dma(reason="small strided constant loads"):
        with nc.Block() as block:
            @block.tensor
            def _(e):
                for f in pe.prog:
                    f(e)

            @block.scalar
            def _(e):
                for f in act.prog:
                    f(e)

            @block.vector
            def _(e):
                for f in dve.prog:
                    f(e)

            @block.gpsimd
            def _(e):
                for f in pool.prog:
                    f(e)

            @block.sync
            def _(e):
                for f in sp.prog:
                    f(e)
    es.close()
    return nc


_CACHE = {}


def kernel(**inputs):
    f32 = lambda a: np.ascontiguousarray(np.asarray(a, dtype=np.float32))
    xp = f32(inputs["x_prompt"]); xs = f32(inputs["x_sample"])[0]
    meta = f32(inputs["meta_tokens"])
    rel_bias = f32(inputs["na_rel_bias"])[0]
    shared = {
        "w_in": f32(inputs["w_in"])[0], "w_out": f32(inputs["w_out"])[0],
        "mix_pre": f32(inputs["mix_norm_pre"])[0], "mix_post": f32(inputs["mix_norm_post"])[0],
        "final_norm": f32(inputs["final_norm"]),
        "q_norm": f32(inputs["gqa_q_norm"])[0], "k_norm": f32(inputs["gqa_k_norm"])[0],
        "grp_ab": np.concatenate([f32(inputs["grp_norm_a"])[0], f32(inputs["grp_norm_b"])[0]]),
        "meta_bias": f32(inputs["na_meta_bias"])[0],
    }
    for f in ("ffn1", "ffn2"):
        shared[f + "_wg"] = f32(inputs[f + "_w_gate"])[0]
        shared[f + "_wu"] = f32(inputs[f + "_w_up"])[0]
        shared[f + "_wd"] = f32(inputs[f + "_w_down"])[0]
        shared[f + "_pre"] = f32(inputs[f + "_norm_pre"])[0]
        shared[f + "_post"] = f32(inputs[f + "_norm_post"])[0]
    tall = np.arange(16384)
    rope_all = _rope_table(tall // 64, tall % 64)
    in_maps = []
    for c in range(8):
        x_own = np.zeros((NOWN, D), np.float32)
        x_own[0:NPR] = xp[c]
        g0 = 32 * c - 4
        lo, hi = max(g0, 0), min(g0 + 40, 256)
        x_own[NPR + (lo - g0) * 64:NPR + (hi - g0) * 64] = xs[lo * 64:hi * 64]
        x_own[NQ:NQ + 16] = meta
        prow = np.zeros(NOWN, np.float32); pcol = np.zeros(NOWN, np.float32)
        tp = np.arange(NPR)
        prow[0:NPR] = tp // 64; pcol[0:NPR] = tp % 64
        ts = np.arange(NSC)
        prow[NPR:NQ] = g0 + ts // 64; pcol[NPR:NQ] = ts % 64
        prow[NQ:NQ + 16] = -1.0; pcol[NQ:NQ + 16] = np.arange(16)
        m = dict(shared)
        keep = np.ones(16384, bool)
        keep[2048 * c:2048 * (c + 1)] = False
        m["xs_full"] = np.ascontiguousarray(xs[keep])
        m["rope_full"] = np.ascontiguousarray(rope_all[keep])
        m["x_own"] = x_own
        m["rope_own"] = _rope_table(prow, pcol)
        m["na_tab"] = _na_tables(rel_bias, c)
        in_maps.append(m)
    if "nc" not in _CACHE:
        _CACHE["nc"] = build_program()
    res = run_bass_kernel_spmd(_CACHE["nc"], in_maps, core_ids=list(range(8)))
    y_prompt = np.stack([np.asarray(res.results[c]["y_p"], np.float32) for c in range(8)], 0)
    y_sample = np.concatenate([np.asarray(res.results[c]["y_s"], np.float32) for c in range(8)], 0)[None]
    return (y_prompt, y_sample)
```
